# Optimizing a Trainium2 kernel written in Bass

```python
import math, functools
import jax, jax.numpy as jnp
from jax import lax
import numpy as np

D_MODEL = 2048
BATCH = 16
SEQ = 2048
DEPTH = 1
DEC_BATCH = 32
DEC_SEQ = 32
PAST_LEN = 2048

CHUNK = 64
GLA_HEADS = 4
GLA_QK = D_MODEL // 2
GLA_V = D_MODEL
GLA_DK = GLA_QK // GLA_HEADS
GLA_DV = GLA_V // GLA_HEADS
GLA_GATE_RANK = 16
GLA_TAU = 16.0
ATT_HEADS = 8
ATT_DH = 128
ATT_WIDTH = ATT_HEADS * ATT_DH
BAND_CHUNKS = 8
BAND_PAST = BAND_CHUNKS * CHUNK
REL_CLIP = 128
D_FF = ((8 * D_MODEL // 3 + 127) // 128) * 128
CONV_W = 3
DN_ALPHA = (2.0 * DEPTH) ** 0.25
DN_BETA = (8.0 * DEPTH) ** -0.25
LN_EPS = 1e-5
RMS_EPS = 1e-6
NEG_INF = -1e30
IN_SPLITS = (GLA_QK, GLA_QK, GLA_V, GLA_V, GLA_GATE_RANK, ATT_WIDTH, ATT_WIDTH, ATT_WIDTH, D_MODEL, D_MODEL)
IN_WIDTH = 2 * GLA_QK + 2 * GLA_V + GLA_GATE_RANK + 3 * ATT_WIDTH + 2 * D_MODEL

kernel_name = "hybrid_gla_chunkband_convffn_deepnorm_step"


def _split_in(z):
    offs = np.cumsum(np.array(IN_SPLITS))[:-1].tolist()
    return jnp.split(z, offs, axis=-1)


def _layer_norm(x, g, b):
    xf = x.astype(jnp.float32)
    mu = jnp.mean(xf, axis=-1, keepdims=True)
    var = jnp.mean(jnp.square(xf - mu), axis=-1, keepdims=True)
    return ((xf - mu) * lax.rsqrt(var + LN_EPS) * g + b).astype(x.dtype)


def _rel_bias(table, rel):
    return table[:, jnp.clip(rel, -REL_CLIP, REL_CLIP) + REL_CLIP].astype(jnp.float32)


def _gla_recurrence(q, k, v, log_a, s0):
    B, L, H, _ = q.shape
    blk = min(CHUNK, L)
    n = L // blk

    def to_blocks(t):
        return t.reshape(B, n, blk, H, t.shape[-1]).transpose(1, 0, 3, 2, 4)

    qb, kb, vb, ab = to_blocks(q), to_blocks(k), to_blocks(v), to_blocks(log_a)
    causal = jnp.tril(jnp.ones((blk, blk), dtype=bool))

    def step(s, inp):
        qc, kc, vc, ac = inp
        b = jnp.cumsum(ac, axis=-2)
        b_last = b[..., -1:, :]
        q_t = qc * jnp.exp(b)
        k_t = kc * jnp.exp(-b)
        att = jnp.where(causal, jnp.einsum('bhtk,bhsk->bhts', q_t, k_t), 0.0)
        o = jnp.einsum('bhts,bhsv->bhtv', att, vc) + jnp.einsum('bhtk,bhkv->bhtv', q_t, s)
        k_dec = kc * jnp.exp(b_last - b)
        s_new = jnp.exp(b_last[..., 0, :])[..., None] * s + jnp.einsum('bhsk,bhsv->bhkv', k_dec, vc)
        return s_new, o

    s_fin, ob = lax.scan(step, s0, (qb, kb, vb, ab))
    o = ob.transpose(1, 0, 3, 2, 4).reshape(B, L, H, v.shape[-1])
    return o, s_fin


def _band_attention_chunked(q, k, v, rel_table):
    B, S, H, D = q.shape
    nc = S // CHUNK
    nb = BAND_CHUNKS + 1
    shp = (B, nc, CHUNK, H, D)
    pad = jnp.zeros((B, BAND_CHUNKS, CHUNK, H, D), k.dtype)
    kc = jnp.concatenate([pad, k.reshape(shp)], axis=1)
    vc = jnp.concatenate([pad, v.reshape(shp)], axis=1)
    qc = q.reshape(shp)
    scores = jnp.concatenate(
        [jnp.einsum('bnqhd,bnkhd->bnhqk', qc, kc[:, o:o + nc], preferred_element_type=jnp.float32)
         for o in range(nb)], axis=-1)
    koff = jnp.arange(nb * CHUNK) - BAND_CHUNKS * CHUNK
    rel = jnp.arange(CHUNK)[:, None] - koff[None, :]
    scores = scores * (D ** -0.5) + _rel_bias(rel_table, rel)[None, None]
    valid = (jnp.arange(nc)[:, None] * CHUNK + koff[None, :]) >= 0
    scores = jnp.where(valid[None, :, None, None, :], scores, NEG_INF)
    p = jax.nn.softmax(scores, axis=-1).astype(v.dtype)
    out = jnp.einsum('bnhqk,bnkhd->bnqhd', p[..., :CHUNK], vc[:, 0:nc])
    for o in range(1, nb):
        out = out + jnp.einsum('bnhqk,bnkhd->bnqhd', p[..., o * CHUNK:(o + 1) * CHUNK], vc[:, o:o + nc])
    return out.reshape(B, S, H, D)


def _band_attention_step(q, k_all, v_all, rel_table):
    L = q.shape[1]
    n_past = k_all.shape[1] - L
    s = jnp.einsum('bqhd,bkhd->bhqk', q, k_all, preferred_element_type=jnp.float32) * (q.shape[-1] ** -0.5)
    rel = (n_past + jnp.arange(L))[:, None] - jnp.arange(n_past + L)[None, :]
    s = s + _rel_bias(rel_table, rel)[None]
    p = jax.nn.softmax(s, axis=-1).astype(v_all.dtype)
    return jnp.einsum('bhqk,bkhd->bqhd', p, v_all)


def _layer(x, past_k, past_v, s0, conv_prev, prm):
    (w_in, gla_gate_up, gla_gate_b, gla_norm_g, att_rel_bias, merge_b, w_br_gla, w_br_att, w_out,
     ln1_g, ln1_b, w_ffn_up, ffn_conv_w, ffn_conv_b, w_ffn_down, ln2_g, ln2_b) = prm
    f32 = jnp.float32
    B, L, _ = x.shape
    z = x @ w_in
    q_g, k_g, v_g, r_g, a_lo, q_a, k_a, v_a, m_g, m_a = _split_in(z)

    log_a = jax.nn.log_sigmoid((a_lo @ gla_gate_up + gla_gate_b).astype(f32)) / GLA_TAU
    hq = lambda t, d: t.reshape(B, L, GLA_HEADS, d).astype(f32)
    o_gla, s_fin = _gla_recurrence(hq(q_g, GLA_DK) * (GLA_DK ** -0.5), hq(k_g, GLA_DK), hq(v_g, GLA_DV),
                                   hq(log_a, GLA_DK), s0.astype(f32))
    o_gla = o_gla * lax.rsqrt(jnp.mean(jnp.square(o_gla), axis=-1, keepdims=True) + RMS_EPS)
    o_gla = (o_gla.reshape(B, L, GLA_V) * gla_norm_g).astype(x.dtype) * jax.nn.silu(r_g)

    ha = lambda t: t.reshape(B, L, ATT_HEADS, ATT_DH)
    q_a, k_a, v_a = ha(q_a), ha(k_a), ha(v_a)
    if past_k is None:
        o_att = _band_attention_chunked(q_a, k_a, v_a, att_rel_bias)
        keep = min(BAND_PAST, L)
        new_k, new_v = k_a[:, L - keep:], v_a[:, L - keep:]
    else:
        k_all = jnp.concatenate([past_k.astype(k_a.dtype), k_a], axis=1)
        v_all = jnp.concatenate([past_v.astype(v_a.dtype), v_a], axis=1)
        o_att = _band_attention_step(q_a, k_all, v_all, att_rel_bias)
        new_k, new_v = k_a, v_a
    o_att = o_att.reshape(B, L, ATT_WIDTH).astype(x.dtype)

    mix = (jax.nn.sigmoid(m_g + merge_b[0]) * (o_gla @ w_br_gla)
           + jax.nn.sigmoid(m_a + merge_b[1]) * (o_att @ w_br_att)) @ w_out
    h = _layer_norm(DN_ALPHA * x + mix, ln1_g, ln1_b)

    ug = h @ w_ffn_up
    u, g = ug[..., :D_FF], ug[..., D_FF:]
    u_ext = jnp.concatenate([conv_prev.astype(u.dtype), u], axis=1)
    uc = ffn_conv_b + u_ext[:, 0:L] * ffn_conv_w[0]
    for i in range(1, CONV_W):
        uc = uc + u_ext[:, i:i + L] * ffn_conv_w[i]
    ffn = (jax.nn.gelu(uc) * g) @ w_ffn_down
    y = _layer_norm(DN_ALPHA * h + ffn, ln2_g, ln2_b)
    return y, new_k, new_v, s_fin.astype(x.dtype), u_ext[:, -(CONV_W - 1):]


def setup_inputs(seed: int = 0) -> dict:
    key = jax.random.key(seed)
    ks = jax.random.split(key, 24)
    nrm = lambda k, shape, s: jax.random.normal(k, shape, jnp.float32) * s
    n_cache = min(BAND_PAST, PAST_LEN)
    return {
        "x_prompt": nrm(ks[0], (BATCH, SEQ, D_MODEL), 1.0),
        "x_sample": nrm(ks[1], (DEC_BATCH, DEC_SEQ, D_MODEL), 1.0),
        "cache_att_k": nrm(ks[2], (DEPTH, DEC_BATCH, n_cache, ATT_HEADS, ATT_DH), 1.0),
        "cache_att_v": nrm(ks[3], (DEPTH, DEC_BATCH, n_cache, ATT_HEADS, ATT_DH), 1.0),
        "state_gla": nrm(ks[4], (DEPTH, DEC_BATCH, GLA_HEADS, GLA_DK, GLA_DV), 0.5),
        "state_ffn_conv": nrm(ks[5], (DEPTH, DEC_BATCH, CONV_W - 1, D_FF), 1.0),
        "w_in": nrm(ks[6], (DEPTH, D_MODEL, IN_WIDTH), D_MODEL ** -0.5),
        "gla_gate_up": nrm(ks[7], (DEPTH, GLA_GATE_RANK, GLA_QK), GLA_GATE_RANK ** -0.5),
        "gla_gate_b": nrm(ks[8], (DEPTH, GLA_QK), 0.1),
        "gla_norm_g": 1.0 + nrm(ks[9], (DEPTH, GLA_V), 0.05),
        "att_rel_bias": nrm(ks[10], (DEPTH, ATT_HEADS, 2 * REL_CLIP + 1), 0.1),
        "merge_b": nrm(ks[11], (DEPTH, 2, D_MODEL), 0.1),
        "w_br_gla": nrm(ks[12], (DEPTH, GLA_V, D_MODEL), GLA_V ** -0.5),
        "w_br_att": nrm(ks[13], (DEPTH, ATT_WIDTH, D_MODEL), ATT_WIDTH ** -0.5),
        "w_out": nrm(ks[14], (DEPTH, D_MODEL, D_MODEL), DN_BETA * D_MODEL ** -0.5),
        "ln1_g": 1.0 + nrm(ks[15], (DEPTH, D_MODEL), 0.05),
        "ln1_b": nrm(ks[16], (DEPTH, D_MODEL), 0.02),
        "w_ffn_up": nrm(ks[17], (DEPTH, D_MODEL, 2 * D_FF), D_MODEL ** -0.5),
        "ffn_conv_w": nrm(ks[18], (DEPTH, CONV_W, D_FF), CONV_W ** -0.5),
        "ffn_conv_b": nrm(ks[19], (DEPTH, D_FF), 0.02),
        "w_ffn_down": nrm(ks[20], (DEPTH, D_FF, D_MODEL), DN_BETA * D_FF ** -0.5),
        "ln2_g": 1.0 + nrm(ks[21], (DEPTH, D_MODEL), 0.05),
        "ln2_b": nrm(ks[22], (DEPTH, D_MODEL), 0.02),
    }


def reference(x_prompt, x_sample, cache_att_k, cache_att_v, state_gla, state_ffn_conv,
              w_in, gla_gate_up, gla_gate_b, gla_norm_g, att_rel_bias, merge_b,
              w_br_gla, w_br_att, w_out, ln1_g, ln1_b, w_ffn_up, ffn_conv_w, ffn_conv_b,
              w_ffn_down, ln2_g, ln2_b):
    xp, xs = x_prompt, x_sample
    bp = xp.shape[0]
    kp, vp, sp, cp = [], [], [], []
    ksm, vsm, ssm, csm = [], [], [], []
    for l in range(DEPTH):
        prm = (w_in[l], gla_gate_up[l], gla_gate_b[l], gla_norm_g[l], att_rel_bias[l], merge_b[l],
               w_br_gla[l], w_br_att[l], w_out[l], ln1_g[l], ln1_b[l], w_ffn_up[l], ffn_conv_w[l],
               ffn_conv_b[l], w_ffn_down[l], ln2_g[l], ln2_b[l])
        s0_p = jnp.zeros((bp, GLA_HEADS, GLA_DK, GLA_DV), jnp.float32)
        conv0_p = jnp.zeros((bp, CONV_W - 1, D_FF), xp.dtype)
        xp, k1, v1, s1, c1 = _layer(xp, None, None, s0_p, conv0_p, prm)
        kp.append(k1); vp.append(v1); sp.append(s1); cp.append(c1)
        xs, k2, v2, s2, c2 = _layer(xs, cache_att_k[l], cache_att_v[l], state_gla[l], state_ffn_conv[l], prm)
        ksm.append(k2); vsm.append(v2); ssm.append(s2); csm.append(c2)
    return (xp, xs,
            jnp.stack(kp), jnp.stack(vp), jnp.stack(sp), jnp.stack(cp),
            jnp.stack(ksm), jnp.stack(vsm), jnp.stack(ssm), jnp.stack(csm))
```

```python
import math
import numpy as np
import concourse.bass as bass
import concourse.mybir as mybir
from concourse.bass_utils import run_bass_kernel_spmd

F32 = mybir.dt.float32
BF16 = mybir.dt.bfloat16
AF = mybir.ActivationFunctionType
ALU = mybir.AluOpType

D = 2048
KC = 16
T = 256
NCORES = 8
DEBUG = False
STOP = None
USE_SCRATCH = True
NSLOT = 6
STOP_TILE = 0
DFF = 5504
NFC = 43
INW = 13328
ALPHA = 2.0 ** 0.25
LN_EPS = 1e-5
RMS_EPS = 1e-6
ATT_SCALE = 128.0 ** -0.5
SQRT128 = 128.0 ** 0.5
LN16 = math.log(1.0 / 16.0)
OQG, OKG, OVG, ORG, OALO, OQA, OKA, OVA, OMG, OMA = 0, 1024, 2048, 4096, 6144, 6160, 7184, 8208, 9232, 11280


PHASE = ["setup"]
LAST_P = [None]
NAME2TAG = {}


class Buf:
    __slots__ = ("name", "lw", "rd", "sem", "dcount", "excl")

    def __init__(self, name, excl=False):
        self.name = name
        self.excl = excl
        self.lw = None
        self.rd = []
        self.sem = None
        self.dcount = 0


class Ins:
    __slots__ = ("eng", "fn", "deps", "is_dma", "need_inc", "cnt", "sembuf", "cum", "tag")

    def __init__(self, eng, fn, is_dma):
        self.tag = PHASE[0]
        self.eng = eng
        self.fn = fn
        self.deps = []
        self.is_dma = is_dma
        self.need_inc = False
        self.cnt = 0
        self.sembuf = None
        self.cum = 0


class Prog:
    ENGS = ("pe", "act", "dve", "pool", "sp")

    def __init__(self):
        self.lists = {e: [] for e in self.ENGS}
        self.dma_bufs = []

    def _rec(self, ins, reads, writes):
        deps = []
        for b in reads:
            if b.lw is not None:
                deps.append((b.lw, True))
            if b.excl:
                last = {}
                for r in b.rd:
                    if r.eng != ins.eng:
                        last[r.eng] = r
                for r in last.values():
                    deps.append((r, True))
        for b in writes:
            if b.lw is not None:
                deps.append((b.lw, False))
            last = {}
            for r in b.rd:
                if r.is_dma:
                    deps.append((r, False))
                else:
                    last[r.eng] = r
            for r in last.values():
                deps.append((r, False))
        seen = set()
        for d, raw in deps:
            if d is ins or id(d) in seen:
                continue
            if (not d.is_dma) and d.eng == ins.eng and ins.eng == "pe":
                continue
            seen.add(id(d))
            ins.deps.append(d)
            if not d.is_dma:
                d.need_inc = True
        for b in reads:
            b.rd.append(ins)
        for b in writes:
            b.lw = ins
            b.rd = []
        self.lists[ins.eng].append(ins)

    def op(self, eng, fn, reads=(), writes=()):
        ins = Ins(eng, fn, False)
        self._rec(ins, reads, writes)
        return ins

    def dma(self, eng, fn, sembuf, reads=(), writes=()):
        ins = Ins(eng, fn, True)
        ins.sembuf = sembuf
        if sembuf.dcount == 0:
            self.dma_bufs.append(sembuf)
        sembuf.dcount += 16
        ins.cum = sembuf.dcount
        self._rec(ins, reads, writes)
        return ins


def build_program(n_pseq=2, plen=2048, n_sseq=4):
    nc = bass.Bass("TRN2", target_bir_lowering=False)
    P = Prog()
    LAST_P[0] = P
    NT = plen // T
    KEEP = min(512, plen)
    SL = 32
    has_sample = n_sseq > 0
    assert n_sseq in (0, 4)

    def din(name, shape, dt=F32):
        return nc.dram_tensor(name, list(shape), dt, kind="ExternalInput").ap()

    def dout(name, shape, dt=F32):
        return nc.dram_tensor(name, list(shape), dt, kind="ExternalOutput").ap()

    xp = din("xp", [n_pseq, plen, D])
    xs = din("xs", [128, D])
    ck = din("ck", [4, 512, 1024])
    cv = din("cv", [4, 512, 1024])
    sgi = din("sg", [4, 4, 256, 512])
    sci = din("sc", [4, 2, DFF])
    w_in = din("w_in", [D, INW])
    gate_up = din("gate_up", [16, 1024])
    gate_b = din("gate_b", [1024])
    gnorm = din("gnorm", [D])
    merge_b = din("merge_b", [2, D])
    w_brg = din("w_brg", [D, D])
    w_bra = din("w_bra", [1024, D])
    w_out = din("w_out", [D, D])
    ln1g = din("ln1g", [D]); ln1b = din("ln1b", [D])
    w_up = din("w_up", [D, 2 * DFF])
    convw = din("convw", [3, DFF]); convb = din("convb", [DFF])
    w_dn = din("w_dn", [DFF, D])
    ln2g = din("ln2g", [D]); ln2b = din("ln2b", [D])
    biasP_d = din("biasP", [128, 8 * 5 * 128])
    maskP_d = din("maskP", [128, 5 * 128])
    biasSc_d = din("biasSc", [128, 8 * 4 * 32])
    biasSn_d = din("biasSn", [128, 8 * 128])
    maskSn_d = din("maskSn", [128, 128])
    ident_d = din("ident", [128, 128])
    cmaskP_d = din("cmaskP", [128, 128])
    cmaskS_d = din("cmaskS", [128, 128])
    rowmask_d = din("rowmask", [128, 4])

    y_p = dout("y_p", [n_pseq, plen, D])
    y_s = dout("y_s", [128, D])
    kp_o = dout("kp", [n_pseq, KEEP, 1024])
    vp_o = dout("vp", [n_pseq, KEEP, 1024])
    sgp_o = dout("sgp", [n_pseq, 4, 256, 512])
    cp_o = dout("cp", [n_pseq, 2, DFF])
    ks_o = dout("ks", [128, 1024])
    vs_o = dout("vs", [128, 1024])
    sgs_o = dout("sgs", [4, 4, 256, 512])
    cs_o = dout("cs", [4, 2, DFF])
    dbg_o = dout("dbg", [128, 24, 128], BF16) if DEBUG else None
    dbg2_o = dout("dbg2", [128, 10752], BF16) if DEBUG else None
    dbg3_o = dout("dbg3", [128, 512], F32) if DEBUG else None

    w_in_v = w_in.rearrange("(kc p) n -> p kc n", p=128)
    w_brg_v = w_brg.rearrange("(kc p) n -> p kc n", p=128)
    w_bra_v = w_bra.rearrange("(kc p) n -> p kc n", p=128)
    w_out_v = w_out.rearrange("(kc p) n -> p kc n", p=128)
    w_up_v = w_up.rearrange("(kc p) n -> p kc n", p=128)
    w_dn_v = w_dn.rearrange("(kc p) n -> p kc n", p=128)

    from contextlib import ExitStack
    es = ExitStack()

    def sb(name, shape, dt):
        return es.enter_context(nc.sbuf_tensor("s_" + name, list(shape), dt))

    wring = sb("wring", [128, NSLOT, 4096], BF16)
    S32 = sb("S32", [128, 4, 2, 512], F32)
    Sbf = sb("Sbf", [128, 4, 2, 512], BF16)
    kvring = sb("kvring", [128, 12288], BF16)
    Kring = kvring[:, 0:6144].rearrange("p (s h t) -> p s h t", s=3, h=8)
    Vring = kvring[:, 6144:12288].rearrange("p (s b f) -> p s b f", s=3, b=2)
    KTc = kvring[:, 0:4096].rearrange("p (h t) -> p h t", h=8)
    Vc = kvring[:, 4096:8192].rearrange("p (b f) -> p b f", b=4)
    Kst = kvring[:, 8192:12288].rearrange("p (b f) -> p b f", b=4)
    biasT = sb("biasT", [128, 8 * 5 * 128], BF16)
    biasPv = biasT[:, :].rearrange("p (h b q) -> p h b q", h=8, b=5)
    biasScv = biasT[:, 0:1024].rearrange("p (h b q) -> p h b q", h=8, b=4)
    biasSnv = biasT[:, 1024:2048].rearrange("p (h q) -> p h q", h=8)
    ident_bf = sb("ident_bf", [128, 128], BF16)
    ident_f = sb("ident_f", [128, 128], F32)
    ones_bf = sb("ones_bf", [128, 128], BF16)
    cmaskP = sb("cmaskP", [128, 128], F32)
    cmaskS = sb("cmaskS", [128, 128], F32)
    rowmask = sb("rowmask", [128, 4], F32)
    gbneg = sb("gbneg", [128, 8], F32)
    par16 = sb("par16", [128, 16, 5], F32)
    convp = sb("convp", [128, NFC, 4], F32)
    u_prev = sb("u_prev", [128, NFC, 4, 2], F32)
    gup_bf = sb("gup_bf", [16, 1024], BF16)
    cst = sb("cst", [128, 8], F32)
    gb_t = sb("gb_t", [128, D], F32)
    R1 = sb("R1", [128, 12800], BF16)
    actT = R1[:, 0:NFC * T].rearrange("p (c t) -> p c t", c=NFC)
    o1 = 0

    def carve(n_bf16):
        nonlocal o1
        v = R1[:, o1:o1 + n_bf16]
        o1 += n_bf16
        return v
    c32 = carve(1024).bitcast(F32).rearrange("p (c t) -> p c t", c=2)
    eA = carve(1024).bitcast(F32).rearrange("p (c t) -> p c t", c=2)
    eB = carve(1024).bitcast(F32).rearrange("p (c t) -> p c t", c=2)
    q_t = carve(512).rearrange("p (c t) -> p c t", c=2)
    k_t = carve(512).rearrange("p (c t) -> p c t", c=2)
    k_dec = carve(512).rearrange("p (c t) -> p c t", c=2)
    v_h = carve(1024).rearrange("p (s f) -> p s f", s=2)
    sr_h = carve(1024).rearrange("p (s f) -> p s f", s=2)
    attm = carve(128)
    kd_tok = carve(1024).rearrange("p (i k) -> p i k", i=4)
    og = carve(512)
    alo_bf = carve(256)
    q_aT = carve(2048).rearrange("p (h t) -> p h t", h=8)
    assert o1 <= 12800, o1
    R2 = sb("R2", [128, KC, T], BF16)
    xT = R2
    hT = R2
    mergedT = sb("mergedT", [128, KC, T], BF16)
    R3 = sb("R3", [128, 6144], BF16)
    o_glaT = R3[:, 0:4096].rearrange("p (c t) -> p c t", c=16)
    o_attT = R3[:, 4096:6144].rearrange("p (c t) -> p c t", c=8)
    ffn_tmp = R3[:, 0:6144].bitcast(F32)
    qm = R3[:, 4096:5120].rearrange("p (i c t) -> p i c t", i=4, c=2)
    h32 = sb("h32", [128, 2, D], F32)
    btmp = h32[:, 1, :]
    xbf = sb("xbf", [128, D], BF16)
    kTn = xbf[:, 0:1024].rearrange("p (h t) -> p h t", h=8)
    vNew = xbf[:, 1024:2048]
    pT = sb("pT", [128, 2, 5, 128], BF16)
    rc_t = sb("rc_t", [128, 2, 128], F32)
    sg_t = sb("sg_t", [128, 2, 2, T], F32)
    kvst = sb("kvst", [128, 2, 512], F32)
    small = sb("small", [128, 256], F32)
    stat = sb("stat", [128, 8, 6], F32)
    Sstage = None

    ps = [es.enter_context(nc.psum_tensor(f"ps{i}", [128, 512], F32)) for i in range(8)]
    ps_buf = [Buf(f"ps{i}", excl=True) for i in range(8)]
    rr = [0]

    def bank():
        i = rr[0] % 8
        rr[0] += 1
        assert ps_buf[i].lw is None or len(ps_buf[i].rd) > 0, f"psum bank {i} reallocated before being read"
        return ps[i], ps_buf[i]

    B = {}

    def buf(name):
        if name not in B:
            B[name] = Buf(name)
        return B[name]

    eng_sem = {}

    def MM(out, lhsT, rhs, start, stop, reads, writes):
        P.op("pe", lambda e: e.matmul(out, lhsT, rhs, start=start, stop=stop), reads, writes)

    def TR(out, in_, ident, reads, writes):
        P.op("pe", lambda e: e.transpose(out, in_, ident), reads, writes)

    def ACT(out, in_, func, reads, writes, bias=None, scale=1.0):
        if bias is None:
            P.op("act", lambda e: e.activation(out=out, in_=in_, func=func, scale=scale), reads, writes)
        else:
            P.op("act", lambda e: e.activation(out=out, in_=in_, func=func, bias=bias, scale=scale), reads, writes)

    def TT(out, in0, in1, op, reads, writes, eng="dve"):
        P.op(eng, lambda e: e.tensor_tensor(out=out, in0=in0, in1=in1, op=op), reads, writes)

    def TS(out, in0, s1, s2, op0, op1, reads, writes, eng="dve"):
        if op1 is None:
            P.op(eng, lambda e: e.tensor_scalar(out=out, in0=in0, scalar1=s1, scalar2=None, op0=op0), reads, writes)
        else:
            P.op(eng, lambda e: e.tensor_scalar(out=out, in0=in0, scalar1=s1, scalar2=s2, op0=op0, op1=op1), reads, writes)

    def STT(out, in0, scalar, in1, op0, op1, reads, writes):
        P.op("dve", lambda e: e.scalar_tensor_tensor(out=out, in0=in0, scalar=scalar, in1=in1, op0=op0, op1=op1), reads, writes)

    def CP(eng, out, in_, reads, writes):
        if eng == "act":
            P.op("act", lambda e: e.copy(out=out, in_=in_), reads, writes)
        else:
            P.op("dve", lambda e: e.tensor_copy(out=out, in_=in_), reads, writes)

    def DMA(eng, out, in_, sembuf, reads, writes, nonc=False):
        if nonc:
            P.dma(eng, lambda e: e.dma_start(out=out, in_=in_, allow_slow_non_contiguous=True), sembuf, reads, writes)
        else:
            P.dma(eng, lambda e: e.dma_start(out=out, in_=in_), sembuf, reads, writes)

    cp_rr = [0]

    def cp_eng():
        cp_rr[0] += 1
        return "act" if cp_rr[0] % 2 else "dve"

    wslot_buf = [Buf(f"wslot{i}") for i in range(NSLOT)]
    wstore_buf = [Buf(f"wstore{i}") for i in range(NSLOT)]
    wcnt = [0]

    scratch = {}
    wtile = [0, 0]

    def wblock(parts):
        views, bufs_ = [], []
        for (kcn, n, src) in parts:
            assert kcn * n <= 4096
            i = wcnt[0] % NSLOT
            wcnt[0] += 1
            sbuf_ = wslot_buf[i]
            flat = wring[:, i, 0:kcn * n]
            view = flat.rearrange("p (k n) -> p k n", n=n)
            pidx = wtile[1]
            wtile[1] += 1
            if (not USE_SCRATCH) or wtile[0] == 0:
                DMA("pool", view, src, sbuf_, [], [sbuf_])
                if USE_SCRATCH:
                    dt_ = nc.dram_tensor(f"wscr{pidx}", [128, kcn * n], BF16).ap()
                    db_ = Buf(f"wscr{pidx}")
                    scratch[pidx] = (dt_, db_)
                    DMA("sp", dt_[:, :], flat, wstore_buf[i], [sbuf_], [db_])
            else:
                dt_, db_ = scratch[pidx]
                DMA("pool", flat, dt_[:, :], sbuf_, [db_], [sbuf_])
            views.append(view)
            bufs_.append(sbuf_)
        return views, bufs_

    def formA(src_v, c0, nk, lhs_fn, lhs_bufs, nsub):
        pbs = [bank() for _ in range(nsub)]
        k0 = 0
        while k0 < nk:
            kn = min(8, nk - k0)
            (w_,), (b_,) = wblock([(kn, 512, src_v[:, k0:k0 + kn, c0:c0 + 512])])
            for s_ in range(nsub):
                pt_, pb_ = pbs[s_]
                for kc in range(kn):
                    kk = k0 + kc
                    MM(pt_[:, :], lhs_fn(kk, s_), w_[:, kc, :], kk == 0, kk == nk - 1, [b_] + lhs_bufs, [pb_])
            k0 += kn
        return pbs

    cb = buf("consts")
    DMA("sp", ident_f[:, :], ident_d[:, :], cb, [], [cb])
    DMA("sp", cmaskP[:, :], cmaskP_d[:, :], cb, [], [cb])
    DMA("sp", cmaskS[:, :], cmaskS_d[:, :], cb, [], [cb])
    DMA("sp", rowmask[:, :], rowmask_d[:, :], cb, [], [cb])
    bt = buf("h32_1")
    DMA("sp", btmp[0:16, 0:1024], gate_up[:, :], bt, [], [bt])
    CP("dve", gup_bf[:, :], btmp[0:16, 0:1024], [bt], [cb, bt])
    P.op("dve", lambda e: e.memset(cst[:, 0:1], 1.0), [], [cb])
    P.op("dve", lambda e: e.memset(cst[:, 1:2], LN16), [], [cb])
    P.op("dve", lambda e: e.memset(cst[:, 2:3], RMS_EPS), [], [cb])
    P.op("dve", lambda e: e.memset(cst[:, 3:4], LN_EPS), [], [cb])
    P.op("dve", lambda e: e.memset(cst[:, 4:5], 0.0), [], [cb])
    P.op("dve", lambda e: e.memset(ones_bf[:, :], 1.0), [], [cb])
    CP("dve", ident_bf[:, :], ident_f[:, :], [cb], [cb])
    DMA("sp", btmp[0:1, 0:1024], gate_b.rearrange("(o n) -> o n", o=1), bt, [], [bt])
    pt_, pb_ = bank()
    for c in range(8):
        TR(pt_[:, c:c + 1], btmp[0:1, c * 128:(c + 1) * 128], ident_f[0:1, 0:1], [bt, cb], [pb_])
    TS(gbneg[:, :], pt_[:, 0:8], -1.0, None, ALU.mult, None, [pb_], [cb])
    DMA("sp", btmp[0:1, 0:D], gnorm.rearrange("(o n) -> o n", o=1), bt, [], [bt])
    DMA("sp", btmp[1:3, 0:D], merge_b[:, :], bt, [], [bt])
    DMA("sp", btmp[3:4, 0:D], ln1g.rearrange("(o n) -> o n", o=1), bt, [], [bt])
    DMA("sp", btmp[4:5, 0:D], ln1b.rearrange("(o n) -> o n", o=1), bt, [], [bt])
    pt_, pb_ = bank()
    ptv_ = pt_[:, 0:80].rearrange("p (c r) -> p c r", r=5)
    for c in range(16):
        TR(ptv_[:, c, :], btmp[0:5, c * 128:(c + 1) * 128], ident_f[0:5, 0:5], [bt, cb], [pb_])
    CP("dve", par16[:, :, :], ptv_, [pb_], [cb])
    cstage = R1[:, :].bitcast(F32)
    r1b = buf("gla_tmp")
    DMA("sp", cstage[0:3, 0:DFF], convw[:, :], r1b, [], [r1b])
    DMA("sp", cstage[3:4, 0:DFF], convb.rearrange("(o n) -> o n", o=1), r1b, [], [r1b])
    pt_, pb_ = bank()
    ptv_ = pt_[:, 0:4 * NFC].rearrange("p (c r) -> p c r", r=4)
    for c in range(NFC):
        TR(ptv_[:, c, :], cstage[0:4, c * 128:(c + 1) * 128], ident_f[0:4, 0:4], [r1b, cb], [pb_])
    CP("dve", convp[:, :, :], ptv_, [pb_], [cb, r1b])
    lnscr = {}
    gbb = buf("gb_t")
    for nm_, src_ in (("g1", ln1g), ("b1", ln1b), ("g2", ln2g), ("b2", ln2b)):
        dt_ = nc.dram_tensor(f"lnscr_{nm_}", [128, D], F32).ap()
        db_ = Buf(f"lnscr_{nm_}")
        DMA("sp", gb_t[:, :], src_.partition_broadcast(128), gbb, [], [gbb])
        DMA("sp", dt_[:, :], gb_t[:, :], gbb, [gbb], [db_])
        lnscr[nm_] = (dt_, db_)
    bb = buf("biasT")
    if n_pseq > 0:
        for h in range(8):
            DMA("sp", btmp[:, 0:640], biasP_d[:, h * 640:(h + 1) * 640], bt, [], [bt])
            DMA("sp", btmp[:, 640:1280], maskP_d[:, :], bt, [], [bt])
            STT(biasT[:, h * 640:(h + 1) * 640], btmp[:, 0:640], SQRT128, btmp[:, 640:1280], ALU.mult, ALU.add,
                [bt], [bb, bt])

    hb = [buf("h32_0"), buf("h32_1")]
    sm = buf("small")

    def layernorm_all(ns, gname, bname, k00, after=None, mid=None):
        for s in range(ns):
            hs = h32[:, s, :]
            k0 = k00 + 8 * s
            for i in range(4):
                P.op("dve", lambda e, i=i, hs=hs: e.bn_stats(out=stat[:, i, :], in_=hs[:, i * 512:(i + 1) * 512]), [hb[s]], [sm])
            P.op("dve", lambda e, k0=k0: e.bn_aggr(out=small[:, k0:k0 + 2], in_=stat[:, 0:4, :]), [sm], [sm])
            ACT(small[:, k0 + 2:k0 + 3], small[:, k0 + 1:k0 + 2], AF.Ln, [sm, cb], [sm], bias=cst[:, 3:4])
            ACT(small[:, k0 + 3:k0 + 4], small[:, k0 + 2:k0 + 3], AF.Exp, [sm], [sm], scale=-0.5)
            STT(small[:, k0 + 4:k0 + 5], small[:, k0:k0 + 1], -1.0, small[:, k0 + 3:k0 + 4], ALU.mult, ALU.mult, [sm], [sm])
            ACT(hs, hs, AF.Identity, [hb[s], sm], [hb[s]], bias=small[:, k0 + 4:k0 + 5], scale=small[:, k0 + 3:k0 + 4])
        if mid is not None:
            mid()
        gd, gdb = lnscr[gname]
        DMA("sp", gb_t[:, :], gd[:, :], gbb, [gdb], [gbb])
        for s in range(ns):
            TT(h32[:, s, :], h32[:, s, :], gb_t[:, :], ALU.mult, [hb[s], gbb], [hb[s]])
        bd, bdb = lnscr[bname]
        DMA("sp", gb_t[:, :], bd[:, :], gbb, [bdb], [gbb])
        for s in range(ns):
            TT(h32[:, s, :], h32[:, s, :], gb_t[:, :], ALU.add, [hb[s], gbb], [hb[s]])
            if after is not None:
                after(s)

    xT_b = buf("R2")
    xbf_b = buf("xbf")
    S_b = buf("S32")
    Sbf_b = buf("Sbf")
    gla_b = buf("gla_tmp")
    vh_b = buf("v_h"); srh_b = buf("sr_h")
    attm_b = buf("attm"); kd_b = buf("kd_tok"); og_b = buf("og")
    oglaT_b = buf("o_glaT"); oattT_b = buf("o_attT"); qm_b = oattT_b
    qaT_b = buf("q_aT")
    K_b = [buf(f"Kring{i}") for i in range(3)]
    V_b = [buf(f"Vring{i}") for i in range(3)]
    pT_b = [buf("pT0"), buf("pT1")]
    rc_b = [buf("rc0"), buf("rc1")]
    mrg_b = buf("mergedT")
    sgt_b = buf("sg_t")
    act_b = buf("actT")
    ffn_b = [buf("ffn_tmp0"), buf("ffn_tmp1"), buf("ffn_tmp2")]
    up_b = buf("u_prev")
    kvst_b = [buf(f"kvst{i}") for i in range(2)]
    kvst_rr = [0]
    newkv_b = xbf_b

    def conv_out(dst2d, nsg):
        rows = nsg * 2
        fb0, fb1, fb2 = ffn_b
        for half in range(2):
            c0, c1 = (0, 22) if half == 0 else (22, NFC)
            for g0 in range(c0, c1, 4):
                g1 = min(g0 + 4, c1)
                pt_, pb_ = bank()
                for c in range(g0, g1):
                    TR(pt_[0:rows, (c - g0) * 128:(c - g0 + 1) * 128],
                       u_prev[:, c, 0:nsg, :].rearrange("p s t -> p (s t)"), ident_f[:, :], [up_b, cb], [pb_])
                CP("dve", ffn_tmp[0:rows, (g0 - c0) * 128:(g1 - c0) * 128], pt_[0:rows, 0:(g1 - g0) * 128], [pb_],
                   [fb0, fb1, fb2, oglaT_b, oattT_b])
            DMA("sp", dst2d[:, c0 * 128:c1 * 128], ffn_tmp[0:rows, 0:(c1 - c0) * 128], fb0, [fb0, fb1, fb2, oglaT_b, oattT_b], [])

    tcount = [0]

    def tile(kind, seq, ti):
        tcount[0] += 1
        wtile[0] = tcount[0] - 1
        wtile[1] = 0
        STOP = globals()["STOP"] if tcount[0] > STOP_TILE else None
        is_s = kind == "s"
        TT_ = 128 if is_s else T
        NS = TT_ // 128
        if is_s:
            x_rows = [xs[0:128, :]]
            y_rows = [y_s[0:128, :]]
            segs = [(i * SL, SL) for i in range(4)]
        else:
            t0 = ti * T
            x_rows = [xp[seq, t0 + s * 128: t0 + (s + 1) * 128, :] for s in range(NS)]
            y_rows = [y_p[seq, t0 + s * 128: t0 + (s + 1) * 128, :] for s in range(NS)]
            segs = [(0, 128)]
        nseg = len(segs)
        first = (not is_s) and ti == 0
        last = (not is_s) and ti == NT - 1
        kv_out = is_s or (t0 >= plen - KEEP)
        slot3 = (ti % 3) if not is_s else 0

        PHASE[0] = "A0"
        for s in range(NS):
            DMA("pool", xbf[:, :], x_rows[s], xbf_b, [], [xbf_b])
            for g4 in range(4):
                pt, pb = bank()
                ptb = pt[:, 0:256].bitcast(BF16).rearrange("p (i t) -> p i t", i=4)
                for i in range(4):
                    fc = g4 * 4 + i
                    TR(ptb[:, i, :], xbf[:, fc * 128:(fc + 1) * 128], ident_bf[:, :], [xbf_b, cb], [pb])
                CP("act", xT[:, g4 * 4:(g4 + 1) * 4, s * 128:(s + 1) * 128], ptb, [pb], [xT_b])

        if STOP == "A0":
            return
        PHASE[0] = "A1"
        (slot,), (sbuf_,) = wblock([(KC, 16, w_in_v[:, :, OALO:OALO + 16])])
        pt, pb = bank()
        for kc in range(KC):
            MM(pt[0:16, 0:TT_], slot[:, kc, 0:16], xT[:, kc, 0:TT_], kc == 0, kc == KC - 1, [sbuf_, xT_b], [pb])
        CP("act", alo_bf[0:16, 0:TT_], pt[0:16, 0:TT_], [pb], [gla_b, act_b])

        if STOP == "A1":
            return
        if first:
            P.op("dve", lambda e: e.memset(S32[:, :, :, :], 0.0), [], [S_b])
            P.op("dve", lambda e: e.memset(Sbf[:, :, :, :], 0.0), [], [Sbf_b])
            P.op("dve", lambda e: e.memset(u_prev[:, :, :, :], 0.0), [], [up_b])
        if is_s:
            h32f = h32[:, :, :].rearrange("p s d -> p (s d)")
            sci2 = sci.rearrange("s t f -> (s t) f")
            HALF = 22 * 128
            DMA("sp", h32f[0:8, 0:HALF], sci2[:, 0:HALF], hb[0], [], [hb[0], hb[1]])
            DMA("sp", h32f[32:40, 0:DFF - HALF], sci2[:, HALF:DFF], hb[0], [], [hb[0], hb[1]])
            pt_, pb_ = bank()
            ptv_ = pt_[:, 0:8 * NFC].rearrange("p (c r) -> p c r", r=8)
            for c in range(NFC):
                if c < 22:
                    TR(ptv_[:, c, :], h32f[0:8, c * 128:(c + 1) * 128], ident_f[0:8, 0:8], [hb[0], hb[1], cb], [pb_])
                else:
                    TR(ptv_[:, c, :], h32f[32:40, (c - 22) * 128:(c - 21) * 128], ident_f[32:40, 32:40],
                       [hb[0], hb[1], cb], [pb_])
            CP("dve", u_prev[:, :, :, :].rearrange("p c s t -> p c (s t)"), ptv_, [pb_], [up_b, hb[0], hb[1]])

        PHASE[0] = "A2"
        for h in range(4):
            PHASE[0] = "A2.qk"
            wqk, bqk = wblock([(KC, 256, w_in_v[:, :, OQG + 256 * h: OQG + 256 * (h + 1)]),
                                 (KC, 256, w_in_v[:, :, OKG + 256 * h: OKG + 256 * (h + 1)])])
            pq, pqb = bank()
            pk, pkb = bank()
            pqv = pq[:, 0:2 * TT_].rearrange("p (c t) -> p c t", c=2)
            pkv = pk[:, 0:2 * TT_].rearrange("p (c t) -> p c t", c=2)
            for j in range(4):
                dst, dstb = (pqv, pqb) if j < 2 else (pkv, pkb)
                for kc in range(KC):
                    MM(dst[:, j % 2, :], wqk[j // 2][:, kc, (j % 2) * 128:(j % 2 + 1) * 128], xT[:, kc, 0:TT_], kc == 0, kc == KC - 1,
                       [bqk[j // 2], xT_b], [dstb])
            PHASE[0] = "A2.gate"
            pg, pgb = bank()
            pgv = pg[:, 0:2 * TT_].rearrange("p (c t) -> p c t", c=2)
            for c in range(2):
                fc = 2 * h + c
                MM(pgv[:, c, :], gup_bf[0:16, fc * 128:(fc + 1) * 128], alo_bf[0:16, 0:TT_], True, True,
                   [cb, gla_b], [pgb])
            for c in range(2):
                fc = 2 * h + c
                ACT(eA[:, c, 0:TT_], pgv[:, c, :], AF.Exp, [pgb, cb], [gla_b], bias=gbneg[:, fc:fc + 1], scale=-1.0)
            ACT(eB[:, :, 0:TT_], eA[:, :, 0:TT_], AF.Ln, [gla_b, cb], [gla_b], bias=cst[:, 0:1], scale=1.0)
            if not is_s:
                seglist = [(b * 128, 128) for b in range(NS)]
            else:
                seglist = segs
            P.op("dve", lambda e: e.memset(eA[:, :, 0:TT_], 1.0), [gla_b], [gla_b])
            for c in range(2):
                for (s0, sl) in seglist:
                    P.op("dve", lambda e, c=c, s0=s0, sl=sl: e.tensor_tensor_scan(
                        out=c32[:, c, s0:s0 + sl], data0=eA[:, c, s0:s0 + sl], data1=eB[:, c, s0:s0 + sl],
                        initial=0.0, op0=ALU.mult, op1=ALU.add), [gla_b], [gla_b])
            nsl = len(seglist)
            nbv = small[:, 16:16 + 2 * nsl].rearrange("p (c s) -> p c s", c=2)
            elv = small[:, 32:32 + 2 * nsl].rearrange("p (c s) -> p c s", c=2)
            for si, (s0, sl) in enumerate(seglist):
                TS(nbv[:, :, si], c32[:, :, s0 + sl - 1], -1.0 / 16.0, None, ALU.mult, None, [gla_b], [sm])
            ACT(small[:, 32:32 + 2 * nsl], small[:, 16:16 + 2 * nsl], AF.Exp, [sm], [sm])
            ACT(eA[:, :, 0:TT_], c32[:, :, 0:TT_], AF.Exp, [gla_b, cb], [gla_b], bias=cst[:, 1:2], scale=-1.0 / 16.0)
            TT(q_t[:, :, 0:TT_], pqv, eA[:, :, 0:TT_], ALU.mult, [pqb, gla_b], [gla_b])
            ACT(eB[:, :, 0:TT_], c32[:, :, 0:TT_], AF.Exp, [gla_b], [gla_b], scale=1.0 / 16.0)
            TT(k_t[:, :, 0:TT_], pkv, eB[:, :, 0:TT_], ALU.mult, [pkb, gla_b], [gla_b])
            for c in range(2):
                for si, (s0, sl) in enumerate(seglist):
                    ACT(eA[:, c, s0:s0 + sl], c32[:, c, s0:s0 + sl], AF.Exp, [gla_b, sm], [gla_b],
                        bias=nbv[:, c, si:si + 1], scale=1.0 / 16.0)
            TT(k_dec[:, :, 0:TT_], pkv, eA[:, :, 0:TT_], ALU.mult, [pkb, gla_b], [gla_b])
            PHASE[0] = "A2.v"
            xlhs = lambda kk, s_: xT[:, kk, s_ * 128:(s_ + 1) * 128]
            pbs_ = formA(w_in_v, OVG + 512 * h, KC, xlhs, [xT_b], NS)
            for s in range(NS):
                pt, pb = pbs_[s]
                CP("act", v_h[:, s, :], pt[:, :], [pb], [vh_b])
            PHASE[0] = "A2.r"
            pbs_ = formA(w_in_v, ORG + 512 * h, KC, xlhs, [xT_b], NS)
            for s in range(NS):
                pt, pb = pbs_[s]
                ACT(sr_h[:, s, :], pt[:, :], AF.Silu, [pb], [srh_b])
            if is_s:
                for i in range(4):
                    DMA("sp", S32[:, i, :, :], sgi[i, h].rearrange("(c p) v -> p c v", p=128), S_b, [], [S_b])
                for i in range(4):
                    CP(cp_eng(), Sbf[:, i, :, :], S32[:, i, :, :], [S_b], [Sbf_b])
                P.op("dve", lambda e: e.memset(qm[:, :, :, :], 0.0), [], [qm_b])
                for i, (s0, sl) in enumerate(segs):
                    CP(cp_eng(), qm[:, i, :, s0:s0 + sl], q_t[:, :, s0:s0 + sl], [gla_b], [qm_b])
            for blk in range(NS):
                cols = slice(blk * 128, (blk + 1) * 128)
                PHASE[0] = "A2.sc"
                psc, pscb = bank()
                for c in range(2):
                    MM(psc[:, 0:128], k_t[:, c, cols], q_t[:, c, cols], c == 0, c == 1, [gla_b], [pscb])
                TT(attm[:, :], psc[:, 0:128], (cmaskS if is_s else cmaskP)[:, :], ALU.mult, [pscb, cb], [attm_b])
                PHASE[0] = "A2.tr"
                ptr, ptrb = bank()
                ptrv = ptr[:, 0:128].bitcast(BF16).rearrange("p (c k) -> p c k", c=2)
                for c in range(2):
                    TR(ptrv[:, c, :], k_dec[:, c, cols], ident_bf[:, :], [gla_b, cb], [ptrb])
                if not is_s:
                    CP("act", kd_tok[:, 0, :], ptr[:, 0:128].bitcast(BF16), [ptrb], [kd_b])
                else:
                    for i in range(4):
                        TS(kd_tok[:, i, :], ptr[:, 0:128].bitcast(BF16), rowmask[:, i:i + 1], None, ALU.mult, None,
                           [ptrb, cb], [kd_b])
                PHASE[0] = "A2.o"
                po, pob = bank()
                MM(po[:, :], attm[:, :], v_h[:, blk, :], True, False, [attm_b, vh_b], [pob])
                if not is_s:
                    for c in range(2):
                        MM(po[:, :], q_t[:, c, cols], Sbf[:, h, c, :], False, c == 1, [gla_b, Sbf_b], [pob])
                else:
                    for i in range(4):
                        for c in range(2):
                            MM(po[:, :], qm[:, i, c, :], Sbf[:, i, c, :], False, (i == 3 and c == 1),
                               [qm_b, Sbf_b], [pob])
                PHASE[0] = "A2.tr2"
                P.op("dve", lambda e, po=po: e.bn_stats(out=stat[:, 4, :], in_=po[:, :]), [pob], [sm])
                P.op("dve", lambda e: e.bn_aggr(out=small[:, 48:50], in_=stat[:, 4:5, :]), [sm], [sm])
                STT(small[:, 50:51], small[:, 48:49], small[:, 48:49], small[:, 49:50], ALU.mult, ALU.add, [sm], [sm])
                ACT(small[:, 51:52], small[:, 50:51], AF.Ln, [sm, cb], [sm], bias=cst[:, 2:3])
                ACT(small[:, 52:53], small[:, 51:52], AF.Exp, [sm], [sm], scale=-0.5)
                STT(og[:, :], po[:, :], small[:, 52:53], sr_h[:, blk, :], ALU.mult, ALU.mult, [pob, sm, srh_b], [og_b])
                pt2, pt2b = bank()
                pt2v = pt2[:, 0:256].bitcast(BF16).rearrange("p (i t) -> p i t", i=4)
                for i in range(4):
                    TR(pt2v[:, i, :], og[:, i * 128:(i + 1) * 128], ident_bf[:, :], [og_b, cb], [pt2b])
                for i in range(4):
                    fcg = 4 * h + i
                    ACT(o_glaT[:, fcg, cols], pt2v[:, i, :], AF.Copy, [pt2b, cb], [oglaT_b], scale=par16[:, fcg, 0:1])
                PHASE[0] = "A2.upd"
                for i in range(nseg):
                    sidx = i if is_s else h
                    for c in range(2):
                        pu, pub = bank()
                        MM(pu[:, :], kd_tok[:, i, c * 128:(c + 1) * 128], v_h[:, blk, :], True, True, [kd_b, vh_b], [pub])
                        si = i if is_s else blk
                        STT(S32[:, sidx, c, :], S32[:, sidx, c, :], elv[:, c, si:si + 1], pu[:, :], ALU.mult, ALU.add,
                            [S_b, sm, pub], [S_b])
                        CP("act", Sbf[:, sidx, c, :], S32[:, sidx, c, :], [S_b], [Sbf_b])
            if is_s:
                for i in range(4):
                    DMA("sp", sgs_o[i, h].rearrange("(c p) v -> p c v", p=128), S32[:, i, :, :], S_b, [S_b], [])
        if last:
            for i in range(4):
                DMA("sp", sgp_o[seq, i].rearrange("(c p) v -> p c v", p=128), S32[:, i, :, :], S_b, [S_b], [])

        if DEBUG and is_s:
            DMA("sp", dbg2_o[:, 3072:8320], R1[:, 3072:8320], gla_b, [gla_b, vh_b, srh_b, attm_b, kd_b, og_b], [])
            DMA("sp", dbg3_o[:, :], R1[:, 0:1024].bitcast(F32), gla_b, [gla_b], [])
        if STOP == "A2":
            return
        PHASE[0] = "B"
        xlhs = lambda kk, s_: xT[:, kk, s_ * 128:(s_ + 1) * 128]
        for hp in range(4):
            (slot,), (sbuf_,) = wblock([(KC, 256, w_in_v[:, :, OQA + 256 * hp: OQA + 256 * (hp + 1)])])
            pt, pb = bank()
            ptv = pt[:, 0:2 * TT_].rearrange("p (c t) -> p c t", c=2)
            for j in range(2):
                for kc in range(KC):
                    MM(ptv[:, j, :], slot[:, kc, j * 128:(j + 1) * 128], xT[:, kc, 0:TT_],
                       kc == 0, kc == KC - 1, [sbuf_, xT_b], [pb])
            hh = hp * 2
            CP(cp_eng(), q_aT[:, hh:hh + 2, 0:TT_], ptv, [pb], [qaT_b])
        if STOP == "Bq":
            return
        for hp in range(4):
            (slot,), (sbuf_,) = wblock([(KC, 256, w_in_v[:, :, OKA + 256 * hp: OKA + 256 * (hp + 1)])])
            pt, pb = bank()
            ptv = pt[:, 0:2 * TT_].rearrange("p (c t) -> p c t", c=2)
            for j in range(2):
                for kc in range(KC):
                    MM(ptv[:, j, :], slot[:, kc, j * 128:(j + 1) * 128], xT[:, kc, 0:TT_],
                       kc == 0, kc == KC - 1, [sbuf_, xT_b], [pb])
            hh = hp * 2
            if is_s:
                CP(cp_eng(), kTn[:, hh:hh + 2, :], ptv, [pb], [newkv_b])
            else:
                CP(cp_eng(), Kring[:, slot3, hh:hh + 2, :], ptv, [pb], [K_b[slot3]])
            if kv_out:
                for s in range(NS):
                    pt, pb = bank()
                    for kc in range(KC):
                        MM(pt[:, 0:256], xT[:, kc, s * 128:(s + 1) * 128], slot[:, kc, :], kc == 0, kc == KC - 1,
                           [sbuf_, xT_b], [pb])
                    ki = kvst_rr[0] % 2
                    kvst_rr[0] += 1
                    CP(cp_eng(), kvst[:, ki, 0:256], pt[:, 0:256], [pb], [kvst_b[ki]])
                    if is_s:
                        dst = ks_o[:, hp * 256:(hp + 1) * 256]
                    else:
                        r0 = t0 - (plen - KEEP) + s * 128
                        dst = kp_o[seq, r0:r0 + 128, hp * 256:(hp + 1) * 256]
                    DMA("sp", dst, kvst[:, ki, 0:256], kvst_b[ki], [kvst_b[ki]], [])
        if STOP == "Bk":
            return
        for cbk in range(2):
            pbs_ = formA(w_in_v, OVA + 512 * cbk, KC, xlhs, [xT_b], NS)
            for s in range(NS):
                pt, pb = pbs_[s]
                ce = cp_eng()
                if is_s:
                    CP(ce, vNew[:, cbk * 512:(cbk + 1) * 512], pt[:, :], [pb], [newkv_b])
                else:
                    CP(ce, Vring[:, slot3, s, cbk * 512:(cbk + 1) * 512], pt[:, :], [pb], [V_b[slot3]])
                if kv_out:
                    ki = kvst_rr[0] % 2
                    kvst_rr[0] += 1
                    CP(ce, kvst[:, ki, :], pt[:, :], [pb], [kvst_b[ki]])
                    if is_s:
                        dst = vs_o[:, cbk * 512:(cbk + 1) * 512]
                    else:
                        r0 = t0 - (plen - KEEP) + s * 128
                        dst = vp_o[seq, r0:r0 + 128, cbk * 512:(cbk + 1) * 512]
                    DMA("sp", dst, kvst[:, ki, :], kvst_b[ki], [kvst_b[ki]], [])

        if STOP == "B":
            return
        PHASE[0] = "B2"
        if not is_s:
            items = [(pr, h) for pr in range(NS) for h in range(8)]

            def stage1(k, pr, h):
                gp = ti * 2 + pr
                blocks = [(b, gp - 4 + b) for b in range(5) if gp - 4 + b >= 0]
                pcols = slice(pr * 128, (pr + 1) * 128)
                bX, bXb = bank()
                bY, bYb = bank()
                low = [(b, g) for (b, g) in blocks if b < 4]
                if low:
                    b0 = low[0][0]
                    MM(bX[:, b0 * 128:512], ident_bf[:, :], biasPv[:, h, b0:4, :], True, False, [cb, bb], [bXb])
                    for (b, g) in low:
                        sl3 = (g // 2) % 3
                        MM(bX[:, b * 128:(b + 1) * 128], Kring[:, sl3, h, (g % 2) * 128:(g % 2 + 1) * 128],
                           q_aT[:, h, pcols], False, b == low[-1][0], [K_b[sl3], qaT_b], [bXb])
                MM(bY[:, 0:128], ident_bf[:, :], biasPv[:, h, 4, :], True, False, [cb, bb], [bYb])
                g = gp
                sl3 = (g // 2) % 3
                MM(bY[:, 0:128], Kring[:, sl3, h, (g % 2) * 128:(g % 2 + 1) * 128], q_aT[:, h, pcols], False, True,
                   [K_b[sl3], qaT_b], [bYb])
                pi = k % 2
                if low:
                    b0 = low[0][0]
                    ACT(pT[:, pi, b0:4, :], bX[:, b0 * 128:512].rearrange("p (b q) -> p b q", q=128), AF.Exp,
                        [bXb], [pT_b[pi]], scale=ATT_SCALE)
                ACT(pT[:, pi, 4, :], bY[:, 0:128], AF.Exp, [bYb], [pT_b[pi]], scale=ATT_SCALE)
                return (blocks, pcols, pi)

            def stage2(k, pr, h, st):
                blocks, pcols, pi = st
                bZ, bZb = bank()
                nb_ = len(blocks)
                for n, (b, g) in enumerate(blocks):
                    sl3 = (g // 2) % 3
                    MM(bZ[:, 0:128], Vring[:, sl3, g % 2, h * 128:(h + 1) * 128], pT[:, pi, b, :], n == 0, n == nb_ - 1,
                       [V_b[sl3], pT_b[pi]], [bZb])
                for n, (b, g) in enumerate(blocks):
                    MM(bZ[:, 128:256], ones_bf[:, :], pT[:, pi, b, :], n == 0, n == nb_ - 1, [cb, pT_b[pi]], [bZb])
                P.op("dve", lambda e: e.reciprocal(out=rc_t[:, pi, :], in_=bZ[:, 128:256]), [bZb], [rc_b[pi]])
                TT(o_attT[:, h, pcols], bZ[:, 0:128], rc_t[:, pi, :], ALU.mult, [bZb, rc_b[pi]], [oattT_b])

            prev = None
            for k, (pr, h) in enumerate(items):
                st = stage1(k, pr, h)
                if prev is not None:
                    stage2(*prev)
                prev = (k, pr, h, st)
            stage2(*prev)
        else:
            sb_b = buf("biasS")
            DMA("sp", btmp[:, 0:1024], biasSc_d[:, :], bt, [], [bt])
            TS(biasT[:, 0:1024], btmp[:, 0:1024], SQRT128, None, ALU.mult, None, [bt], [bb, bt])
            for h in range(8):
                DMA("sp", btmp[:, 0:128], biasSn_d[:, h * 128:(h + 1) * 128], bt, [], [bt])
                DMA("sp", btmp[:, 128:256], maskSn_d[:, :], bt, [], [bt])
                STT(biasT[:, 1024 + h * 128: 1024 + (h + 1) * 128], btmp[:, 0:128], SQRT128, btmp[:, 128:256],
                    ALU.mult, ALU.add, [bt], [bb, bt])
            kc_b = buf("KTc"); vc_b = buf("Vc"); kst_b = buf("Kst")
            for sq in range(4):
                DMA("pool", Kst[:, :, :], ck[sq].rearrange("(b p) f -> p b f", p=128), kst_b, [], [kst_b] + K_b + V_b)
                DMA("pool", Vc[:, :, :], cv[sq].rearrange("(b p) f -> p b f", p=128), vc_b, [], [vc_b] + K_b + V_b)
                for h in range(8):
                    pt, pb = bank()
                    ptv = pt[:, 0:256].bitcast(BF16).rearrange("p (b t) -> p b t", b=4)
                    for b in range(4):
                        TR(ptv[:, b, :], Kst[:, b, h * 128:(h + 1) * 128], ident_bf[:, :], [kst_b, cb], [pb])
                    CP(cp_eng(), KTc[:, h, :], pt[:, 0:256].bitcast(BF16), [pb], [kc_b])
                qc = slice(sq * SL, (sq + 1) * SL)
                for h in range(8):
                    bX, bXb = bank()
                    bY, bYb = bank()
                    MM(bX[:, 0:128], ident_bf[:, :], biasScv[:, h, :, :], True, False, [cb, bb], [bXb])
                    for b in range(4):
                        MM(bX[:, b * 32:(b + 1) * 32], KTc[:, h, b * 128:(b + 1) * 128], q_aT[:, h, qc], False, b == 3,
                           [kc_b, qaT_b], [bXb])
                    MM(bY[:, 0:32], ident_bf[:, :], biasSnv[:, h, qc], True, False, [cb, bb], [bYb])
                    MM(bY[:, 0:32], kTn[:, h, :], q_aT[:, h, qc], False, True, [newkv_b, qaT_b], [bYb])
                    pi = h % 2
                    pTs = pT[:, pi, 0, :].rearrange("p (b q) -> p b q", b=4)
                    ACT(pTs, bX[:, 0:128].rearrange("p (b q) -> p b q", b=4), AF.Exp, [bXb], [pT_b[pi]], scale=ATT_SCALE)
                    ACT(pT[:, pi, 1, 0:32], bY[:, 0:32], AF.Exp, [bYb], [pT_b[pi]], scale=ATT_SCALE)
                    bZ, bZb = bank()
                    for b in range(4):
                        MM(bZ[:, 0:32], Vc[:, b, h * 128:(h + 1) * 128], pTs[:, b, :], b == 0, False, [vc_b, pT_b[pi]], [bZb])
                    MM(bZ[:, 0:32], vNew[:, h * 128:(h + 1) * 128], pT[:, pi, 1, 0:32], False, True, [newkv_b, pT_b[pi]], [bZb])
                    for b in range(4):
                        MM(bZ[:, 128:160], ones_bf[:, :], pTs[:, b, :], b == 0, False, [cb, pT_b[pi]], [bZb])
                    MM(bZ[:, 128:160], ones_bf[:, :], pT[:, pi, 1, 0:32], False, True, [cb, pT_b[pi]], [bZb])
                    P.op("dve", lambda e, bZ=bZ, pi=pi: e.reciprocal(out=rc_t[:, pi, 0:32], in_=bZ[:, 128:160]),
                         [bZb], [rc_b[pi]])
                    TT(o_attT[:, h, qc], bZ[:, 0:32], rc_t[:, pi, 0:32], ALU.mult, [bZb, rc_b[pi]], [oattT_b])

        if DEBUG and is_s:
            DMA("sp", dbg_o[:, 0:16, :], o_glaT[:, :, 0:128], oglaT_b, [oglaT_b, oattT_b], [])
            DMA("sp", dbg_o[:, 16:24, :], o_attT[:, :, 0:128], oglaT_b, [oglaT_b, oattT_b], [])
        if STOP == "B2":
            return
        for s in range(NS):
            DMA("sp", h32[:, s, :], x_rows[s], hb[s], [], [hb[s]])

        PHASE[0] = "C"
        for j in range(8):
            wm, bm = wblock([(KC, 256, w_in_v[:, :, OMG + 256 * j: OMG + 256 * (j + 1)]),
                                (KC, 256, w_in_v[:, :, OMA + 256 * j: OMA + 256 * (j + 1)])])
            for w in range(2):
                pt, pb = bank()
                ptv = pt[:, 0:2 * TT_].rearrange("p (c t) -> p c t", c=2)
                for i in range(2):
                    for kc in range(KC):
                        MM(ptv[:, i, :], wm[w][:, kc, i * 128:(i + 1) * 128], xT[:, kc, 0:TT_],
                           kc == 0, kc == KC - 1, [bm[w], xT_b], [pb])
                for i in range(2):
                    ACT(sg_t[:, w, i, 0:TT_], ptv[:, i, :], AF.Sigmoid, [pb, cb], [sgt_b],
                        bias=par16[:, 2 * j + i, 1 + w:2 + w])
            wb, bwb = wblock([(KC, 256, w_brg_v[:, :, 256 * j: 256 * (j + 1)]),
                                (8, 256, w_bra_v[:, :, 256 * j: 256 * (j + 1)])])
            pg_, pgb_ = bank()
            pa_, pab_ = bank()
            pgv_ = pg_[:, 0:2 * TT_].rearrange("p (c t) -> p c t", c=2)
            pav_ = pa_[:, 0:2 * TT_].rearrange("p (c t) -> p c t", c=2)
            for i in range(2):
                for kc in range(KC):
                    MM(pgv_[:, i, :], wb[0][:, kc, i * 128:(i + 1) * 128], o_glaT[:, kc, 0:TT_], kc == 0, kc == KC - 1,
                       [bwb[0], oglaT_b], [pgb_])
            for i in range(2):
                for kc in range(8):
                    MM(pav_[:, i, :], wb[1][:, kc, i * 128:(i + 1) * 128], o_attT[:, kc, 0:TT_], kc == 0, kc == 7,
                       [bwb[1], oattT_b], [pab_])
            TT(sg_t[:, 0, :, 0:TT_], sg_t[:, 0, :, 0:TT_], pgv_, ALU.mult, [sgt_b, pgb_], [sgt_b])
            TT(sg_t[:, 1, :, 0:TT_], sg_t[:, 1, :, 0:TT_], pav_, ALU.mult, [sgt_b, pab_], [sgt_b])
            TT(mergedT[:, 2 * j:2 * j + 2, 0:TT_], sg_t[:, 0, :, 0:TT_], sg_t[:, 1, :, 0:TT_], ALU.add, [sgt_b], [mrg_b])

        if STOP == "C":
            return
        PHASE[0] = "D"
        for n in range(4):
            mlhs = lambda kk, s_: mergedT[:, kk, s_ * 128:(s_ + 1) * 128]
            pbs_ = formA(w_out_v, 512 * n, KC, mlhs, [mrg_b], NS)
            for s in range(NS):
                pt, pb = pbs_[s]
                hsl = h32[:, s, n * 512:(n + 1) * 512]
                STT(hsl, hsl, ALPHA, pt[:, :], ALU.mult, ALU.add, [hb[s], pb], [hb[s]])
        def ln1_transposes():
            for s in range(NS):
                for g4 in range(4):
                    pt, pb = bank()
                    ptv = pt[:, :].rearrange("p (i t) -> p i t", i=4)
                    for i in range(4):
                        fc = g4 * 4 + i
                        TR(ptv[:, i, :], h32[:, s, fc * 128:(fc + 1) * 128], ident_f[:, :], [hb[s], cb], [pb])
                    for i in range(4):
                        fc = g4 * 4 + i
                        ACT(hT[:, fc, s * 128:(s + 1) * 128], ptv[:, i, :], AF.Identity, [pb, cb], [xT_b],
                            bias=par16[:, fc, 4:5], scale=par16[:, fc, 3:4])

        layernorm_all(NS, "g1", "b1", 64, mid=ln1_transposes)
        if STOP == "D":
            return
        PHASE[0] = "E"
        L = SL if is_s else TT_
        for j in range(22):
            ncj = 2 if j < 21 else 1
            wu, bwu = wblock([(KC, ncj * 128, w_up_v[:, :, 256 * j: 256 * j + ncj * 128]),
                                (KC, ncj * 128, w_up_v[:, :, DFF + 256 * j: DFF + 256 * j + ncj * 128])])
            pu_, pub_ = bank()
            pg_, pgb_ = bank()
            puv = pu_[:, 0:2 * TT_].rearrange("p (c t) -> p c t", c=2)
            pgv_ = pg_[:, 0:2 * TT_].rearrange("p (c t) -> p c t", c=2)
            for i in range(ncj):
                for kc in range(KC):
                    MM(puv[:, i, :], wu[0][:, kc, i * 128:(i + 1) * 128], hT[:, kc, 0:TT_], kc == 0, kc == KC - 1,
                       [bwu[0], xT_b], [pub_])
            for i in range(ncj):
                for kc in range(KC):
                    MM(pgv_[:, i, :], wu[1][:, kc, i * 128:(i + 1) * 128], hT[:, kc, 0:TT_], kc == 0, kc == KC - 1,
                       [bwu[1], xT_b], [pgb_])
            for i in range(ncj):
                fc = 2 * j + i
                fi = fc % 3
                fb = ffn_b[fi]
                base = fi * 1024
                uext = ffn_tmp[:, base: base + nseg * (L + 2)].rearrange("p (s l) -> p s l", s=nseg)
                t1 = ffn_tmp[:, base + 264: base + 264 + TT_].rearrange("p (s l) -> p s l", s=nseg)
                gl = ffn_tmp[:, base + 520: base + 520 + TT_]
                al_w = [oglaT_b, oattT_b] if fc < 3 else []
                al_r = [oglaT_b, oattT_b] if fc >= NFC - 3 else []
                CP("act", uext[:, :, 2:L + 2], puv[:, i, :].rearrange("p (s l) -> p s l", s=nseg), [pub_], [fb] + al_w)
                CP("act", uext[:, :, 0:2], u_prev[:, fc, 0:nseg, :], [up_b], [fb])
                TS(t1, uext[:, :, 0:L], convp[:, fc, 0:1], convp[:, fc, 3:4], ALU.mult, ALU.add, [fb, cb], [fb])
                STT(t1, uext[:, :, 1:L + 1], convp[:, fc, 1:2], t1, ALU.mult, ALU.add, [fb, cb], [fb])
                STT(t1, uext[:, :, 2:L + 2], convp[:, fc, 2:3], t1, ALU.mult, ALU.add, [fb, cb], [fb])
                CP("act", u_prev[:, fc, 0:nseg, :], uext[:, :, L:L + 2], [fb], [up_b])
                ACT(gl, ffn_tmp[:, base + 264: base + 264 + TT_], AF.Gelu_apprx_tanh, [fb], [fb])
                al_a = [gla_b, vh_b, srh_b, attm_b, kd_b, og_b, qaT_b] if fc == 0 else []
                TT(actT[:, fc, 0:TT_], gl, pgv_[:, i, :], ALU.mult, [fb, pgb_] + al_r, [act_b] + al_a)
        if last:
            conv_out(cp_o[seq], 1)
        if is_s:
            conv_out(cs_o.rearrange("s t f -> (s t) f"), 4)

        if STOP == "E":
            return
        PHASE[0] = "F"
        alhs = lambda kk, s_: actT[:, kk, s_ * 128:(s_ + 1) * 128]
        for n in range(4):
            pbs = formA(w_dn_v, 512 * n, NFC, alhs, [act_b], NS)
            for s in range(NS):
                pt, pb = pbs[s]
                hsl = h32[:, s, n * 512:(n + 1) * 512]
                STT(hsl, hsl, ALPHA, pt[:, :], ALU.mult, ALU.add, [hb[s], pb], [hb[s]])
        layernorm_all(NS, "g2", "b2", 80,
                      after=lambda s: DMA("sp", y_rows[s], h32[:, s, :], hb[s], [hb[s]], []))

    if STOP != "setup":
        for seq in range(n_pseq):
            for ti in range(NT):
                tile("p", seq, ti)
        if has_sample:
            tile("s", 0, 0)

    for e in ("pe", "act", "dve"):
        c = 0
        for ins in P.lists[e]:
            if ins.need_inc:
                c += 1
                ins.cnt = c
    sems = {e: es.enter_context(nc.semaphore(f"sem_{e}")) for e in ("pe", "act", "dve")}
    for b_ in P.dma_bufs:
        b_.sem = es.enter_context(nc.semaphore(f"d_{b_.name}"))

    engmap = {"pe": "tensor", "act": "scalar", "dve": "vector", "pool": "gpsimd", "sp": "sync"}

    def emit(ename):
        def body(eng):
            known = {}
            for ins in P.lists[ename]:
                need = {}
                for d in ins.deps:
                    if d.is_dma:
                        key, val, sem = id(d.sembuf), d.cum, d.sembuf.sem
                    else:
                        key, val, sem = d.eng, d.cnt, sems[d.eng]
                    if key not in need or need[key][0] < val:
                        need[key] = (val, sem)
                for key, (val, sem) in need.items():
                    if known.get(key, 0) >= val:
                        continue
                    known[key] = val
                    eng.wait_ge(sem, val)
                r = ins.fn(eng)
                if ename == "pe":
                    NAME2TAG[r.ins.name] = ins.tag
                if ins.is_dma:
                    r.then_inc(ins.sembuf.sem, 16)
                elif ins.need_inc:
                    r.then_inc(sems[ename], 1)
            if ename == "sp":
                for b_ in P.dma_bufs:
                    eng.wait_ge(b_.sem, b_.dcount)
        return body

    with nc.Block() as block:
        block.tensor(emit("pe"))
        block.scalar(emit("act"))
        block.vector(emit("dve"))
        block.gpsimd(emit("pool"))
        block.sync(emit("sp"))
    es.close()
    return nc


def _const_tables():
    ident = np.eye(128, dtype=np.float32)
    s = np.arange(128)
    cmaskP = (s[:, None] <= s[None, :]).astype(np.float32)
    same = (s[:, None] // 32) == (s[None, :] // 32)
    cmaskS = (cmaskP * same).astype(np.float32)
    rowmask = np.zeros((128, 4), np.float32)
    for i in range(4):
        rowmask[i * 32:(i + 1) * 32, i] = 1.0
    j = np.arange(128)[:, None, None]
    b = np.arange(5)[None, :, None]
    q = np.arange(128)[None, None, :]
    kch = 2 * b + j // 64 - 8
    qch = q // 64
    valid = (kch >= qch - 8) & (kch <= qch)
    maskP = np.where(valid, 0.0, -1e30).astype(np.float32).reshape(128, 5 * 128)
    maskSn = np.where(same, 0.0, -1e30).astype(np.float32)
    return ident, cmaskP, cmaskS, rowmask, maskP, maskSn


def _bias_indices():
    j = np.arange(128)[:, None, None]
    b = np.arange(5)[None, :, None]
    q = np.arange(128)[None, None, :]
    relP = q + 512 - 128 * b - j
    idxP = np.clip(relP, -128, 128) + 128
    b4 = np.arange(4)[None, :, None]
    i = np.arange(32)[None, None, :]
    relC = 512 + i - 128 * b4 - j
    idxC = np.clip(relC, -128, 128) + 128
    jj = np.arange(128)[:, None]
    qq = np.arange(128)[None, :]
    relN = (qq % 32) - (jj % 32)
    idxN = np.clip(relN, -128, 128) + 128
    return idxP, idxC, idxN


_NC_CACHE = {}


def _get_nc(key):
    if key not in _NC_CACHE:
        _NC_CACHE[key] = build_program(*key)
    return _NC_CACHE[key]


def run_cores(inputs, n_cores, n_pseq, plen, n_sseq):
    f = lambda a: np.ascontiguousarray(np.asarray(a, dtype=np.float32))
    x_prompt = f(inputs["x_prompt"]); x_sample = f(inputs["x_sample"])
    ident, cmaskP, cmaskS, rowmask, maskP, maskSn = _const_tables()
    idxP, idxC, idxN = _bias_indices()
    tab = f(inputs["att_rel_bias"])[0]
    biasP = np.ascontiguousarray(tab[:, idxP].transpose(1, 0, 2, 3).reshape(128, 8 * 5 * 128))
    biasSc = np.ascontiguousarray(tab[:, idxC].transpose(1, 0, 2, 3).reshape(128, 8 * 4 * 32))
    biasSn = np.ascontiguousarray(tab[:, idxN].transpose(1, 0, 2).reshape(128, 8 * 128))
    shared = {
        "w_in": f(inputs["w_in"])[0], "gate_up": f(inputs["gla_gate_up"])[0], "gate_b": f(inputs["gla_gate_b"])[0],
        "gnorm": f(inputs["gla_norm_g"])[0], "merge_b": f(inputs["merge_b"])[0],
        "w_brg": f(inputs["w_br_gla"])[0], "w_bra": f(inputs["w_br_att"])[0], "w_out": f(inputs["w_out"])[0],
        "ln1g": f(inputs["ln1_g"])[0], "ln1b": f(inputs["ln1_b"])[0], "w_up": f(inputs["w_ffn_up"])[0],
        "convw": f(inputs["ffn_conv_w"])[0], "convb": f(inputs["ffn_conv_b"])[0], "w_dn": f(inputs["w_ffn_down"])[0],
        "ln2g": f(inputs["ln2_g"])[0], "ln2b": f(inputs["ln2_b"])[0],
        "biasP": biasP, "maskP": maskP, "biasSc": biasSc, "biasSn": biasSn, "maskSn": maskSn,
        "ident": ident, "cmaskP": cmaskP, "cmaskS": cmaskS, "rowmask": rowmask,
    }
    ck = f(inputs["cache_att_k"])[0]; cv = f(inputs["cache_att_v"])[0]
    sg = f(inputs["state_gla"])[0]; sc = f(inputs["state_ffn_conv"])[0]
    in_maps = []
    for c in range(n_cores):
        m = dict(shared)
        m["xp"] = np.ascontiguousarray(x_prompt[c * n_pseq:(c + 1) * n_pseq])
        if n_sseq:
            sl = slice(c * 4, (c + 1) * 4)
            m["xs"] = np.ascontiguousarray(x_sample[sl].reshape(128, D))
            m["ck"] = np.ascontiguousarray(ck[sl].reshape(4, 512, 1024))
            m["cv"] = np.ascontiguousarray(cv[sl].reshape(4, 512, 1024))
            m["sg"] = np.ascontiguousarray(sg[sl]); m["sc"] = np.ascontiguousarray(sc[sl])
        else:
            m["xs"] = np.zeros((128, D), np.float32)
            m["ck"] = np.zeros((4, 512, 1024), np.float32); m["cv"] = np.zeros((4, 512, 1024), np.float32)
            m["sg"] = np.zeros((4, 4, 256, 512), np.float32); m["sc"] = np.zeros((4, 2, DFF), np.float32)
        in_maps.append(m)
    nc = _get_nc((n_pseq, plen, n_sseq))
    res = run_bass_kernel_spmd(nc, in_maps, core_ids=list(range(n_cores)))
    R = res.results
    keep = min(512, plen)
    cat = lambda k: np.concatenate([r[k] for r in R], axis=0)
    y_p = cat("y_p")
    y_s = cat("y_s").reshape(n_cores * 4, 32, D)
    kp = cat("kp").reshape(1, n_cores * n_pseq, keep, 8, 128)
    vp = cat("vp").reshape(1, n_cores * n_pseq, keep, 8, 128)
    sgp = cat("sgp")[None]
    cp = cat("cp")[None]
    ks = cat("ks").reshape(1, n_cores * 4, 32, 8, 128)
    vs = cat("vs").reshape(1, n_cores * 4, 32, 8, 128)
    sgs = cat("sgs")[None]
    cs = cat("cs")[None]
    return (y_p, y_s, kp, vp, sgp, cp, ks, vs, sgs, cs)


def kernel(**inputs):
    return run_cores(inputs, NCORES, 2, 2048, 4)
```

```python
import math
import numpy as np
import concourse.bass as bass
import concourse.mybir as mybir
from concourse.bass_utils import run_bass_kernel_spmd

F32 = mybir.dt.float32
BF16 = mybir.dt.bfloat16
AF = mybir.ActivationFunctionType
ALU = mybir.AluOpType

D = 2048
KC = 16
T = 256
NCORES = 8
DEBUG = False
STOP = None
USE_SCRATCH = True
NSLOT = 6
STOP_TILE = 0
DFF = 5504
NFC = 43
INW = 13328
ALPHA = 2.0 ** 0.25
LN_EPS = 1e-5
RMS_EPS = 1e-6
ATT_SCALE = 128.0 ** -0.5
SQRT128 = 128.0 ** 0.5
LN16 = math.log(1.0 / 16.0)
OQG, OKG, OVG, ORG, OALO, OQA, OKA, OVA, OMG, OMA = 0, 1024, 2048, 4096, 6144, 6160, 7184, 8208, 9232, 11280


PHASE = ["setup"]
LAST_P = [None]
NAME2TAG = {}


class Buf:
    __slots__ = ("name", "lw", "rd", "sem", "dcount", "excl", "held")

    def __init__(self, name, excl=False):
        self.name = name
        self.excl = excl
        self.held = False
        self.lw = None
        self.rd = []
        self.sem = None
        self.dcount = 0


class Ins:
    __slots__ = ("eng", "fn", "deps", "is_dma", "need_inc", "cnt", "sembuf", "cum", "tag")

    def __init__(self, eng, fn, is_dma):
        self.tag = PHASE[0]
        self.eng = eng
        self.fn = fn
        self.deps = []
        self.is_dma = is_dma
        self.need_inc = False
        self.cnt = 0
        self.sembuf = None
        self.cum = 0


class Prog:
    ENGS = ("pe", "act", "dve", "pool", "sp")

    def __init__(self):
        self.lists = {e: [] for e in self.ENGS}
        self.dma_bufs = []

    def _rec(self, ins, reads, writes):
        deps = []
        for b in reads:
            if b.lw is not None:
                deps.append((b.lw, True))
            if b.excl:
                last = {}
                for r in b.rd:
                    if r.eng != ins.eng:
                        last[r.eng] = r
                for r in last.values():
                    deps.append((r, True))
        for b in writes:
            if b.lw is not None:
                deps.append((b.lw, False))
            last = {}
            for r in b.rd:
                if r.is_dma:
                    deps.append((r, False))
                else:
                    last[r.eng] = r
            for r in last.values():
                deps.append((r, False))
        seen = set()
        for d, raw in deps:
            if d is ins or id(d) in seen:
                continue
            if (not d.is_dma) and d.eng == ins.eng and ins.eng == "pe":
                continue
            seen.add(id(d))
            ins.deps.append(d)
            if not d.is_dma:
                d.need_inc = True
        for b in reads:
            b.rd.append(ins)
            b.held = False
        for b in writes:
            b.lw = ins
            b.rd = []
        self.lists[ins.eng].append(ins)

    def op(self, eng, fn, reads=(), writes=()):
        ins = Ins(eng, fn, False)
        self._rec(ins, reads, writes)
        return ins

    def dma(self, eng, fn, sembuf, reads=(), writes=()):
        ins = Ins(eng, fn, True)
        ins.sembuf = sembuf
        if sembuf.dcount == 0:
            self.dma_bufs.append(sembuf)
        sembuf.dcount += 16
        ins.cum = sembuf.dcount
        self._rec(ins, reads, writes)
        return ins


def build_program(n_pseq=2, plen=2048, n_sseq=4):
    nc = bass.Bass("TRN2", target_bir_lowering=False)
    P = Prog()
    LAST_P[0] = P
    NT = plen // T
    KEEP = min(512, plen)
    SL = 32
    has_sample = n_sseq > 0
    assert n_sseq in (0, 4)

    def din(name, shape, dt=F32):
        return nc.dram_tensor(name, list(shape), dt, kind="ExternalInput").ap()

    def dout(name, shape, dt=F32):
        return nc.dram_tensor(name, list(shape), dt, kind="ExternalOutput").ap()

    xp = din("xp", [n_pseq, plen, D])
    xs = din("xs", [128, D])
    ck = din("ck", [4, 512, 1024])
    cv = din("cv", [4, 512, 1024])
    sgi = din("sg", [4, 4, 256, 512])
    sci = din("sc", [4, 2, DFF])
    w_in = din("w_in", [D, INW])
    gate_up = din("gate_up", [16, 1024])
    gate_b = din("gate_b", [1024])
    gnorm = din("gnorm", [D])
    merge_b = din("merge_b", [2, D])
    w_brg = din("w_brg", [D, D])
    w_bra = din("w_bra", [1024, D])
    w_out = din("w_out", [D, D])
    ln1g = din("ln1g", [D]); ln1b = din("ln1b", [D])
    w_up = din("w_up", [D, 2 * DFF])
    convw = din("convw", [3, DFF]); convb = din("convb", [DFF])
    w_dn = din("w_dn", [DFF, D])
    ln2g = din("ln2g", [D]); ln2b = din("ln2b", [D])
    biasP_d = din("biasP", [128, 8 * 5 * 128])
    maskP_d = din("maskP", [128, 5 * 128])
    biasSc_d = din("biasSc", [128, 8 * 4 * 32])
    biasSn_d = din("biasSn", [128, 8 * 128])
    maskSn_d = din("maskSn", [128, 128])
    ident_d = din("ident", [128, 128])
    cmaskP_d = din("cmaskP", [128, 128])
    cmaskS_d = din("cmaskS", [128, 128])
    rowmask_d = din("rowmask", [128, 4])

    y_p = dout("y_p", [n_pseq, plen, D])
    y_s = dout("y_s", [128, D])
    kp_o = dout("kp", [n_pseq, KEEP, 1024])
    vp_o = dout("vp", [n_pseq, KEEP, 1024])
    sgp_o = dout("sgp", [n_pseq, 4, 256, 512])
    cp_o = dout("cp", [n_pseq, 2, DFF])
    ks_o = dout("ks", [128, 1024])
    vs_o = dout("vs", [128, 1024])
    sgs_o = dout("sgs", [4, 4, 256, 512])
    cs_o = dout("cs", [4, 2, DFF])
    dbg_o = dout("dbg", [128, 24, 128], BF16) if DEBUG else None
    dbg2_o = dout("dbg2", [128, 10752], BF16) if DEBUG else None
    dbg3_o = dout("dbg3", [128, 512], F32) if DEBUG else None

    w_in_v = w_in.rearrange("(kc p) n -> p kc n", p=128)
    w_brg_v = w_brg.rearrange("(kc p) n -> p kc n", p=128)
    w_bra_v = w_bra.rearrange("(kc p) n -> p kc n", p=128)
    w_out_v = w_out.rearrange("(kc p) n -> p kc n", p=128)
    w_up_v = w_up.rearrange("(kc p) n -> p kc n", p=128)
    w_dn_v = w_dn.rearrange("(kc p) n -> p kc n", p=128)

    from contextlib import ExitStack
    es = ExitStack()

    def sb(name, shape, dt):
        return es.enter_context(nc.sbuf_tensor("s_" + name, list(shape), dt))

    wring = sb("wring", [128, NSLOT, 4096], BF16)
    S32 = sb("S32", [128, 4, 2, 512], F32)
    Sbf = sb("Sbf", [128, 4, 2, 512], BF16)
    kvring = sb("kvring", [128, 12288], BF16)
    Kring = kvring[:, 0:6144].rearrange("p (s h t) -> p s h t", s=3, h=8)
    Vring = kvring[:, 6144:12288].rearrange("p (s b f) -> p s b f", s=3, b=2)
    KTc = kvring[:, 0:4096].rearrange("p (h t) -> p h t", h=8)
    Vc = kvring[:, 4096:8192].rearrange("p (b f) -> p b f", b=4)
    Kst = kvring[:, 8192:12288].rearrange("p (b f) -> p b f", b=4)
    biasT = sb("biasT", [128, 8 * 5 * 128], BF16)
    biasPv = biasT[:, :].rearrange("p (h b q) -> p h b q", h=8, b=5)
    biasScv = biasT[:, 0:1024].rearrange("p (h b q) -> p h b q", h=8, b=4)
    biasSnv = biasT[:, 1024:2048].rearrange("p (h q) -> p h q", h=8)
    ident_bf = sb("ident_bf", [128, 128], BF16)
    ident_f = sb("ident_f", [128, 128], F32)
    ones_bf = sb("ones_bf", [128, 128], BF16)
    cmaskP = sb("cmaskP", [128, 128], F32)
    cmaskS = sb("cmaskS", [128, 128], F32)
    rowmask = sb("rowmask", [128, 4], F32)
    gbneg = sb("gbneg", [128, 8], F32)
    par16 = sb("par16", [128, 16, 5], F32)
    convp = sb("convp", [128, NFC, 4], F32)
    u_prev = sb("u_prev", [128, NFC, 4, 2], F32)
    gup_bf = sb("gup_bf", [16, 1024], BF16)
    cst = sb("cst", [128, 8], F32)
    gb_t = sb("gb_t", [128, D], F32)
    R1 = sb("R1", [128, 12800], BF16)
    actT = R1[:, 0:NFC * T].rearrange("p (c t) -> p c t", c=NFC)
    o1 = 0

    def carve(n_bf16):
        nonlocal o1
        v = R1[:, o1:o1 + n_bf16]
        o1 += n_bf16
        return v
    c32 = carve(1024).bitcast(F32).rearrange("p (c t) -> p c t", c=2)
    eA = carve(1024).bitcast(F32).rearrange("p (c t) -> p c t", c=2)
    eB = carve(1024).bitcast(F32).rearrange("p (c t) -> p c t", c=2)
    q_t = carve(512).rearrange("p (c t) -> p c t", c=2)
    k_t = carve(512).rearrange("p (c t) -> p c t", c=2)
    k_dec = carve(512).rearrange("p (c t) -> p c t", c=2)
    v_h = carve(1024).rearrange("p (s f) -> p s f", s=2)
    sr_h = carve(1024).rearrange("p (s f) -> p s f", s=2)
    attm = carve(128)
    kd_tok = carve(1024).rearrange("p (i k) -> p i k", i=4)
    og = carve(512)
    alo_bf = carve(256)
    q_aT = carve(2048).rearrange("p (h t) -> p h t", h=8)
    assert o1 <= 12800, o1
    R2 = sb("R2", [128, KC, T], BF16)
    xT = R2
    hT = R2
    mergedT = sb("mergedT", [128, KC, T], BF16)
    R3 = sb("R3", [128, 6144], BF16)
    o_glaT = R3[:, 0:4096].rearrange("p (c t) -> p c t", c=16)
    o_attT = R3[:, 4096:6144].rearrange("p (c t) -> p c t", c=8)
    ffn_tmp = R3[:, 0:6144].bitcast(F32)
    qm = R3[:, 4096:5120].rearrange("p (i c t) -> p i c t", i=4, c=2)
    h32 = sb("h32", [128, 2, D], F32)
    btmp = h32[:, 1, :]
    xbf = sb("xbf", [128, D], BF16)
    kTn = xbf[:, 0:1024].rearrange("p (h t) -> p h t", h=8)
    vNew = xbf[:, 1024:2048]
    pT = sb("pT", [128, 2, 5, 128], BF16)
    rc_t = sb("rc_t", [128, 2, 128], F32)
    sg_t = sb("sg_t", [128, 2, 2, T], F32)
    kvst = sb("kvst", [128, 2, 512], F32)
    small = sb("small", [128, 256], F32)
    stat = sb("stat", [128, 8, 6], F32)
    Sstage = None

    ps = [es.enter_context(nc.psum_tensor(f"ps{i}", [128, 512], F32)) for i in range(8)]
    ps_buf = [Buf(f"ps{i}", excl=True) for i in range(8)]
    rr = [0]

    def bank():
        for _ in range(8):
            i = rr[0] % 8
            rr[0] += 1
            if not ps_buf[i].held:
                ps_buf[i].held = True
                return ps[i], ps_buf[i]
        raise RuntimeError("all PSUM banks are held")

    B = {}

    def buf(name):
        if name not in B:
            B[name] = Buf(name)
        return B[name]

    eng_sem = {}

    def MM(out, lhsT, rhs, start, stop, reads, writes):
        P.op("pe", lambda e: e.matmul(out, lhsT, rhs, start=start, stop=stop), reads, writes)

    def TR(out, in_, ident, reads, writes):
        P.op("pe", lambda e: e.transpose(out, in_, ident), reads, writes)

    def ACT(out, in_, func, reads, writes, bias=None, scale=1.0):
        if bias is None:
            P.op("act", lambda e: e.activation(out=out, in_=in_, func=func, scale=scale), reads, writes)
        else:
            P.op("act", lambda e: e.activation(out=out, in_=in_, func=func, bias=bias, scale=scale), reads, writes)

    def TT(out, in0, in1, op, reads, writes, eng="dve"):
        P.op(eng, lambda e: e.tensor_tensor(out=out, in0=in0, in1=in1, op=op), reads, writes)

    def TS(out, in0, s1, s2, op0, op1, reads, writes, eng="dve"):
        if op1 is None:
            P.op(eng, lambda e: e.tensor_scalar(out=out, in0=in0, scalar1=s1, scalar2=None, op0=op0), reads, writes)
        else:
            P.op(eng, lambda e: e.tensor_scalar(out=out, in0=in0, scalar1=s1, scalar2=s2, op0=op0, op1=op1), reads, writes)

    def STT(out, in0, scalar, in1, op0, op1, reads, writes):
        P.op("dve", lambda e: e.scalar_tensor_tensor(out=out, in0=in0, scalar=scalar, in1=in1, op0=op0, op1=op1), reads, writes)

    def CP(eng, out, in_, reads, writes):
        if eng == "act":
            P.op("act", lambda e: e.copy(out=out, in_=in_), reads, writes)
        else:
            P.op("dve", lambda e: e.tensor_copy(out=out, in_=in_), reads, writes)

    def DMA(eng, out, in_, sembuf, reads, writes, nonc=False):
        if nonc:
            P.dma(eng, lambda e: e.dma_start(out=out, in_=in_, allow_slow_non_contiguous=True), sembuf, reads, writes)
        else:
            P.dma(eng, lambda e: e.dma_start(out=out, in_=in_), sembuf, reads, writes)

    cp_rr = [0]

    def cp_eng():
        cp_rr[0] += 1
        return "act" if cp_rr[0] % 2 else "dve"

    wslot_buf = [Buf(f"wslot{i}") for i in range(NSLOT)]
    wstore_buf = [Buf(f"wstore{i}") for i in range(NSLOT)]
    wcnt = [0]

    scratch = {}
    wtile = [0, 0]

    def wblock(parts):
        views, bufs_ = [], []
        for (kcn, n, src) in parts:
            assert kcn * n <= 4096
            i = wcnt[0] % NSLOT
            wcnt[0] += 1
            sbuf_ = wslot_buf[i]
            flat = wring[:, i, 0:kcn * n]
            view = flat.rearrange("p (k n) -> p k n", n=n)
            pidx = wtile[1]
            wtile[1] += 1
            if (not USE_SCRATCH) or wtile[0] == 0:
                DMA("pool", view, src, sbuf_, [], [sbuf_])
                if USE_SCRATCH:
                    dt_ = nc.dram_tensor(f"wscr{pidx}", [128, kcn * n], BF16).ap()
                    db_ = Buf(f"wscr{pidx}")
                    scratch[pidx] = (dt_, db_)
                    DMA("sp", dt_[:, :], flat, wstore_buf[i], [sbuf_], [db_])
            else:
                dt_, db_ = scratch[pidx]
                DMA("pool", flat, dt_[:, :], sbuf_, [db_], [sbuf_])
            views.append(view)
            bufs_.append(sbuf_)
        return views, bufs_

    def formA(src_v, c0, nk, lhs_fn, lhs_bufs, nsub):
        pbs = [bank() for _ in range(nsub)]
        k0 = 0
        while k0 < nk:
            kn = min(8, nk - k0)
            (w_,), (b_,) = wblock([(kn, 512, src_v[:, k0:k0 + kn, c0:c0 + 512])])
            for s_ in range(nsub):
                pt_, pb_ = pbs[s_]
                for kc in range(kn):
                    kk = k0 + kc
                    MM(pt_[:, :], lhs_fn(kk, s_), w_[:, kc, :], kk == 0, kk == nk - 1, [b_] + lhs_bufs, [pb_])
            k0 += kn
        return pbs

    cb = buf("consts")
    DMA("sp", ident_f[:, :], ident_d[:, :], cb, [], [cb])
    DMA("sp", cmaskP[:, :], cmaskP_d[:, :], cb, [], [cb])
    DMA("sp", cmaskS[:, :], cmaskS_d[:, :], cb, [], [cb])
    DMA("sp", rowmask[:, :], rowmask_d[:, :], cb, [], [cb])
    bt = buf("h32_1")
    DMA("sp", btmp[0:16, 0:1024], gate_up[:, :], bt, [], [bt])
    CP("dve", gup_bf[:, :], btmp[0:16, 0:1024], [bt], [cb, bt])
    P.op("dve", lambda e: e.memset(cst[:, 0:1], 1.0), [], [cb])
    P.op("dve", lambda e: e.memset(cst[:, 1:2], LN16), [], [cb])
    P.op("dve", lambda e: e.memset(cst[:, 2:3], RMS_EPS), [], [cb])
    P.op("dve", lambda e: e.memset(cst[:, 3:4], LN_EPS), [], [cb])
    P.op("dve", lambda e: e.memset(cst[:, 4:5], 0.0), [], [cb])
    P.op("dve", lambda e: e.memset(ones_bf[:, :], 1.0), [], [cb])
    CP("dve", ident_bf[:, :], ident_f[:, :], [cb], [cb])
    DMA("sp", btmp[0:1, 0:1024], gate_b.rearrange("(o n) -> o n", o=1), bt, [], [bt])
    pt_, pb_ = bank()
    for c in range(8):
        TR(pt_[:, c:c + 1], btmp[0:1, c * 128:(c + 1) * 128], ident_f[0:1, 0:1], [bt, cb], [pb_])
    TS(gbneg[:, :], pt_[:, 0:8], -1.0, None, ALU.mult, None, [pb_], [cb])
    DMA("sp", btmp[0:1, 0:D], gnorm.rearrange("(o n) -> o n", o=1), bt, [], [bt])
    DMA("sp", btmp[1:3, 0:D], merge_b[:, :], bt, [], [bt])
    DMA("sp", btmp[3:4, 0:D], ln1g.rearrange("(o n) -> o n", o=1), bt, [], [bt])
    DMA("sp", btmp[4:5, 0:D], ln1b.rearrange("(o n) -> o n", o=1), bt, [], [bt])
    pt_, pb_ = bank()
    ptv_ = pt_[:, 0:80].rearrange("p (c r) -> p c r", r=5)
    for c in range(16):
        TR(ptv_[:, c, :], btmp[0:5, c * 128:(c + 1) * 128], ident_f[0:5, 0:5], [bt, cb], [pb_])
    CP("dve", par16[:, :, :], ptv_, [pb_], [cb])
    cstage = R1[:, :].bitcast(F32)
    r1b = buf("gla_tmp")
    DMA("sp", cstage[0:3, 0:DFF], convw[:, :], r1b, [], [r1b])
    DMA("sp", cstage[3:4, 0:DFF], convb.rearrange("(o n) -> o n", o=1), r1b, [], [r1b])
    pt_, pb_ = bank()
    ptv_ = pt_[:, 0:4 * NFC].rearrange("p (c r) -> p c r", r=4)
    for c in range(NFC):
        TR(ptv_[:, c, :], cstage[0:4, c * 128:(c + 1) * 128], ident_f[0:4, 0:4], [r1b, cb], [pb_])
    CP("dve", convp[:, :, :], ptv_, [pb_], [cb, r1b])
    lnscr = {}
    gbb = buf("gb_t")
    for nm_, src_ in (("g1", ln1g), ("b1", ln1b), ("g2", ln2g), ("b2", ln2b)):
        dt_ = nc.dram_tensor(f"lnscr_{nm_}", [128, D], F32).ap()
        db_ = Buf(f"lnscr_{nm_}")
        DMA("sp", gb_t[:, :], src_.partition_broadcast(128), gbb, [], [gbb])
        DMA("sp", dt_[:, :], gb_t[:, :], gbb, [gbb], [db_])
        lnscr[nm_] = (dt_, db_)
    bb = buf("biasT")
    if n_pseq > 0:
        for h in range(8):
            DMA("sp", btmp[:, 0:640], biasP_d[:, h * 640:(h + 1) * 640], bt, [], [bt])
            DMA("sp", btmp[:, 640:1280], maskP_d[:, :], bt, [], [bt])
            STT(biasT[:, h * 640:(h + 1) * 640], btmp[:, 0:640], SQRT128, btmp[:, 640:1280], ALU.mult, ALU.add,
                [bt], [bb, bt])

    hb = [buf("h32_0"), buf("h32_1")]
    sm = buf("small")

    def layernorm_all(ns, gname, bname, k00, after=None, mid=None):
        for s in range(ns):
            hs = h32[:, s, :]
            k0 = k00 + 8 * s
            for i in range(4):
                P.op("dve", lambda e, i=i, hs=hs: e.bn_stats(out=stat[:, i, :], in_=hs[:, i * 512:(i + 1) * 512]), [hb[s]], [sm])
            P.op("dve", lambda e, k0=k0: e.bn_aggr(out=small[:, k0:k0 + 2], in_=stat[:, 0:4, :]), [sm], [sm])
            ACT(small[:, k0 + 2:k0 + 3], small[:, k0 + 1:k0 + 2], AF.Ln, [sm, cb], [sm], bias=cst[:, 3:4])
            ACT(small[:, k0 + 3:k0 + 4], small[:, k0 + 2:k0 + 3], AF.Exp, [sm], [sm], scale=-0.5)
            STT(small[:, k0 + 4:k0 + 5], small[:, k0:k0 + 1], -1.0, small[:, k0 + 3:k0 + 4], ALU.mult, ALU.mult, [sm], [sm])
            ACT(hs, hs, AF.Identity, [hb[s], sm], [hb[s]], bias=small[:, k0 + 4:k0 + 5], scale=small[:, k0 + 3:k0 + 4])
        if mid is not None:
            mid()
        gd, gdb = lnscr[gname]
        DMA("sp", gb_t[:, :], gd[:, :], gbb, [gdb], [gbb])
        for s in range(ns):
            TT(h32[:, s, :], h32[:, s, :], gb_t[:, :], ALU.mult, [hb[s], gbb], [hb[s]])
        bd, bdb = lnscr[bname]
        DMA("sp", gb_t[:, :], bd[:, :], gbb, [bdb], [gbb])
        for s in range(ns):
            TT(h32[:, s, :], h32[:, s, :], gb_t[:, :], ALU.add, [hb[s], gbb], [hb[s]])
            if after is not None:
                after(s)

    xT_b = buf("R2")
    xbf_b = buf("xbf")
    S_b = buf("S32")
    Sbf_b = buf("Sbf")
    gla_b = buf("gla_tmp")
    vh_b = buf("v_h"); srh_b = buf("sr_h")
    attm_b = buf("attm"); kd_b = buf("kd_tok"); og_b = buf("og")
    oglaT_b = buf("o_glaT"); oattT_b = buf("o_attT"); qm_b = oattT_b
    qaT_b = buf("q_aT")
    K_b = [buf(f"Kring{i}") for i in range(3)]
    V_b = [buf(f"Vring{i}") for i in range(3)]
    pT_b = [buf("pT0"), buf("pT1")]
    rc_b = [buf("rc0"), buf("rc1")]
    mrg_b = buf("mergedT")
    sgt_b = buf("sg_t")
    act_b = buf("actT")
    ffn_b = [buf("ffn_tmp0"), buf("ffn_tmp1"), buf("ffn_tmp2")]
    up_b = buf("u_prev")
    kvst_b = [buf(f"kvst{i}") for i in range(2)]
    kvst_rr = [0]
    newkv_b = xbf_b

    def conv_out(dst2d, nsg):
        rows = nsg * 2
        fb0, fb1, fb2 = ffn_b
        for half in range(2):
            c0, c1 = (0, 22) if half == 0 else (22, NFC)
            for g0 in range(c0, c1, 4):
                g1 = min(g0 + 4, c1)
                pt_, pb_ = bank()
                for c in range(g0, g1):
                    TR(pt_[0:rows, (c - g0) * 128:(c - g0 + 1) * 128],
                       u_prev[:, c, 0:nsg, :].rearrange("p s t -> p (s t)"), ident_f[:, :], [up_b, cb], [pb_])
                CP("dve", ffn_tmp[0:rows, (g0 - c0) * 128:(g1 - c0) * 128], pt_[0:rows, 0:(g1 - g0) * 128], [pb_],
                   [fb0, fb1, fb2, oglaT_b, oattT_b])
            DMA("sp", dst2d[:, c0 * 128:c1 * 128], ffn_tmp[0:rows, 0:(c1 - c0) * 128], fb0, [fb0, fb1, fb2, oglaT_b, oattT_b], [])

    tcount = [0]

    def tile(kind, seq, ti):
        tcount[0] += 1
        wtile[0] = tcount[0] - 1
        wtile[1] = 0
        STOP = globals()["STOP"] if tcount[0] > STOP_TILE else None
        is_s = kind == "s"
        TT_ = 128 if is_s else T
        NS = TT_ // 128
        if is_s:
            x_rows = [xs[0:128, :]]
            y_rows = [y_s[0:128, :]]
            segs = [(i * SL, SL) for i in range(4)]
        else:
            t0 = ti * T
            x_rows = [xp[seq, t0 + s * 128: t0 + (s + 1) * 128, :] for s in range(NS)]
            y_rows = [y_p[seq, t0 + s * 128: t0 + (s + 1) * 128, :] for s in range(NS)]
            segs = [(0, 128)]
        nseg = len(segs)
        first = (not is_s) and ti == 0
        last = (not is_s) and ti == NT - 1
        kv_out = is_s or (t0 >= plen - KEEP)
        slot3 = (ti % 3) if not is_s else 0

        PHASE[0] = "A0"
        for s in range(NS):
            DMA("pool", xbf[:, :], x_rows[s], xbf_b, [], [xbf_b])
            for g4 in range(4):
                pt, pb = bank()
                ptb = pt[:, 0:256].bitcast(BF16).rearrange("p (i t) -> p i t", i=4)
                for i in range(4):
                    fc = g4 * 4 + i
                    TR(ptb[:, i, :], xbf[:, fc * 128:(fc + 1) * 128], ident_bf[:, :], [xbf_b, cb], [pb])
                CP("act", xT[:, g4 * 4:(g4 + 1) * 4, s * 128:(s + 1) * 128], ptb, [pb], [xT_b])

        if STOP == "A0":
            return
        PHASE[0] = "A1"
        (slot,), (sbuf_,) = wblock([(KC, 16, w_in_v[:, :, OALO:OALO + 16])])
        pt, pb = bank()
        for kc in range(KC):
            MM(pt[0:16, 0:TT_], slot[:, kc, 0:16], xT[:, kc, 0:TT_], kc == 0, kc == KC - 1, [sbuf_, xT_b], [pb])
        CP("act", alo_bf[0:16, 0:TT_], pt[0:16, 0:TT_], [pb], [gla_b, act_b])

        if STOP == "A1":
            return
        if first:
            P.op("dve", lambda e: e.memset(S32[:, :, :, :], 0.0), [], [S_b])
            P.op("dve", lambda e: e.memset(Sbf[:, :, :, :], 0.0), [], [Sbf_b])
            P.op("dve", lambda e: e.memset(u_prev[:, :, :, :], 0.0), [], [up_b])
        if is_s:
            h32f = h32[:, :, :].rearrange("p s d -> p (s d)")
            sci2 = sci.rearrange("s t f -> (s t) f")
            HALF = 22 * 128
            DMA("sp", h32f[0:8, 0:HALF], sci2[:, 0:HALF], hb[0], [], [hb[0], hb[1]])
            DMA("sp", h32f[32:40, 0:DFF - HALF], sci2[:, HALF:DFF], hb[0], [], [hb[0], hb[1]])
            pt_, pb_ = bank()
            ptv_ = pt_[:, 0:8 * NFC].rearrange("p (c r) -> p c r", r=8)
            for c in range(NFC):
                if c < 22:
                    TR(ptv_[:, c, :], h32f[0:8, c * 128:(c + 1) * 128], ident_f[0:8, 0:8], [hb[0], hb[1], cb], [pb_])
                else:
                    TR(ptv_[:, c, :], h32f[32:40, (c - 22) * 128:(c - 21) * 128], ident_f[32:40, 32:40],
                       [hb[0], hb[1], cb], [pb_])
            CP("dve", u_prev[:, :, :, :].rearrange("p c s t -> p c (s t)"), ptv_, [pb_], [up_b, hb[0], hb[1]])

        PHASE[0] = "A2"
        qk_state = {}

        def emit_qk(hh, part):
            PHASE[0] = "A2.qk"
            if hh not in qk_state:
                wqk_, bqk_ = wblock([(KC, 256, w_in_v[:, :, OQG + 256 * hh: OQG + 256 * (hh + 1)]),
                                     (KC, 256, w_in_v[:, :, OKG + 256 * hh: OKG + 256 * (hh + 1)])])
                pq_, pqb_ = bank()
                pk_, pkb_ = bank()
                qk_state[hh] = [wqk_, bqk_, pq_, pqb_, pk_, pkb_, 0]
            st_ = qk_state[hh]
            wqk_, bqk_, pq_, pqb_, pk_, pkb_, done_ = st_
            assert part == done_
            st_[6] += 1
            pqv_ = pq_[:, 0:2 * TT_].rearrange("p (c t) -> p c t", c=2)
            pkv_ = pk_[:, 0:2 * TT_].rearrange("p (c t) -> p c t", c=2)
            j = part
            dst, dstb = (pqv_, pqb_) if j < 2 else (pkv_, pkb_)
            for kc in range(KC):
                MM(dst[:, j % 2, :], wqk_[j // 2][:, kc, (j % 2) * 128:(j % 2 + 1) * 128], xT[:, kc, 0:TT_], kc == 0, kc == KC - 1,
                   [bqk_[j // 2], xT_b], [dstb])

        def hoist_qk(hh, nparts):
            if hh >= 4:
                return
            for _ in range(nparts):
                d_ = qk_state[hh][6] if hh in qk_state else 0
                if d_ < 4:
                    emit_qk(hh, d_)

        for h in range(4):
            hoist_qk(h, 4)
            _w, _b, pq, pqb, pk, pkb, _d = qk_state[h]
            pqv = pq[:, 0:2 * TT_].rearrange("p (c t) -> p c t", c=2)
            pkv = pk[:, 0:2 * TT_].rearrange("p (c t) -> p c t", c=2)
            PHASE[0] = "A2.gate"
            pg, pgb = bank()
            pgv = pg[:, 0:2 * TT_].rearrange("p (c t) -> p c t", c=2)
            for c in range(2):
                fc = 2 * h + c
                MM(pgv[:, c, :], gup_bf[0:16, fc * 128:(fc + 1) * 128], alo_bf[0:16, 0:TT_], True, True,
                   [cb, gla_b], [pgb])
            for c in range(2):
                fc = 2 * h + c
                ACT(eA[:, c, 0:TT_], pgv[:, c, :], AF.Exp, [pgb, cb], [gla_b], bias=gbneg[:, fc:fc + 1], scale=-1.0)
            ACT(eB[:, :, 0:TT_], eA[:, :, 0:TT_], AF.Ln, [gla_b, cb], [gla_b], bias=cst[:, 0:1], scale=1.0)
            if not is_s:
                seglist = [(b * 128, 128) for b in range(NS)]
            else:
                seglist = segs
            P.op("dve", lambda e: e.memset(eA[:, :, 0:TT_], 1.0), [gla_b], [gla_b])
            for c in range(2):
                for (s0, sl) in seglist:
                    P.op("dve", lambda e, c=c, s0=s0, sl=sl: e.tensor_tensor_scan(
                        out=c32[:, c, s0:s0 + sl], data0=eA[:, c, s0:s0 + sl], data1=eB[:, c, s0:s0 + sl],
                        initial=0.0, op0=ALU.mult, op1=ALU.add), [gla_b], [gla_b])
            nsl = len(seglist)
            nbv = small[:, 16:16 + 2 * nsl].rearrange("p (c s) -> p c s", c=2)
            elv = small[:, 32:32 + 2 * nsl].rearrange("p (c s) -> p c s", c=2)
            for si, (s0, sl) in enumerate(seglist):
                TS(nbv[:, :, si], c32[:, :, s0 + sl - 1], -1.0 / 16.0, None, ALU.mult, None, [gla_b], [sm])
            ACT(small[:, 32:32 + 2 * nsl], small[:, 16:16 + 2 * nsl], AF.Exp, [sm], [sm])
            ACT(eA[:, :, 0:TT_], c32[:, :, 0:TT_], AF.Exp, [gla_b, cb], [gla_b], bias=cst[:, 1:2], scale=-1.0 / 16.0)
            TT(q_t[:, :, 0:TT_], pqv, eA[:, :, 0:TT_], ALU.mult, [pqb, gla_b], [gla_b])
            ACT(eB[:, :, 0:TT_], c32[:, :, 0:TT_], AF.Exp, [gla_b], [gla_b], scale=1.0 / 16.0)
            TT(k_t[:, :, 0:TT_], pkv, eB[:, :, 0:TT_], ALU.mult, [pkb, gla_b], [gla_b])
            for c in range(2):
                for si, (s0, sl) in enumerate(seglist):
                    ACT(eA[:, c, s0:s0 + sl], c32[:, c, s0:s0 + sl], AF.Exp, [gla_b, sm], [gla_b],
                        bias=nbv[:, c, si:si + 1], scale=1.0 / 16.0)
            TT(k_dec[:, :, 0:TT_], pkv, eA[:, :, 0:TT_], ALU.mult, [pkb, gla_b], [gla_b])
            PHASE[0] = "A2.v"
            xlhs = lambda kk, s_: xT[:, kk, s_ * 128:(s_ + 1) * 128]
            pbs_ = formA(w_in_v, OVG + 512 * h, KC, xlhs, [xT_b], NS)
            for s in range(NS):
                pt, pb = pbs_[s]
                CP("act", v_h[:, s, :], pt[:, :], [pb], [vh_b])
            PHASE[0] = "A2.r"
            pbs_ = formA(w_in_v, ORG + 512 * h, KC, xlhs, [xT_b], NS)
            for s in range(NS):
                pt, pb = pbs_[s]
                ACT(sr_h[:, s, :], pt[:, :], AF.Silu, [pb], [srh_b])
            if is_s:
                for i in range(4):
                    DMA("sp", S32[:, i, :, :], sgi[i, h].rearrange("(c p) v -> p c v", p=128), S_b, [], [S_b])
                for i in range(4):
                    CP(cp_eng(), Sbf[:, i, :, :], S32[:, i, :, :], [S_b], [Sbf_b])
                P.op("dve", lambda e: e.memset(qm[:, :, :, :], 0.0), [], [qm_b])
                for i, (s0, sl) in enumerate(segs):
                    CP(cp_eng(), qm[:, i, :, s0:s0 + sl], q_t[:, :, s0:s0 + sl], [gla_b], [qm_b])
            for blk in range(NS):
                cols = slice(blk * 128, (blk + 1) * 128)
                PHASE[0] = "A2.sc"
                psc, pscb = bank()
                for c in range(2):
                    MM(psc[:, 0:128], k_t[:, c, cols], q_t[:, c, cols], c == 0, c == 1, [gla_b], [pscb])
                TT(attm[:, :], psc[:, 0:128], (cmaskS if is_s else cmaskP)[:, :], ALU.mult, [pscb, cb], [attm_b])
                PHASE[0] = "A2.tr"
                ptr, ptrb = bank()
                ptrv = ptr[:, 0:128].bitcast(BF16).rearrange("p (c k) -> p c k", c=2)
                for c in range(2):
                    TR(ptrv[:, c, :], k_dec[:, c, cols], ident_bf[:, :], [gla_b, cb], [ptrb])
                if not is_s:
                    CP("act", kd_tok[:, 0, :], ptr[:, 0:128].bitcast(BF16), [ptrb], [kd_b])
                else:
                    for i in range(4):
                        TS(kd_tok[:, i, :], ptr[:, 0:128].bitcast(BF16), rowmask[:, i:i + 1], None, ALU.mult, None,
                           [ptrb, cb], [kd_b])
                hoist_qk(h + 1, 1)
                PHASE[0] = "A2.o"
                po, pob = bank()
                MM(po[:, :], attm[:, :], v_h[:, blk, :], True, False, [attm_b, vh_b], [pob])
                if not is_s:
                    for c in range(2):
                        MM(po[:, :], q_t[:, c, cols], Sbf[:, h, c, :], False, c == 1, [gla_b, Sbf_b], [pob])
                else:
                    for i in range(4):
                        for c in range(2):
                            MM(po[:, :], qm[:, i, c, :], Sbf[:, i, c, :], False, (i == 3 and c == 1),
                               [qm_b, Sbf_b], [pob])
                PHASE[0] = "A2.tr2"
                P.op("dve", lambda e, po=po: e.bn_stats(out=stat[:, 4, :], in_=po[:, :]), [pob], [sm])
                P.op("dve", lambda e: e.bn_aggr(out=small[:, 48:50], in_=stat[:, 4:5, :]), [sm], [sm])
                STT(small[:, 50:51], small[:, 48:49], small[:, 48:49], small[:, 49:50], ALU.mult, ALU.add, [sm], [sm])
                ACT(small[:, 51:52], small[:, 50:51], AF.Ln, [sm, cb], [sm], bias=cst[:, 2:3])
                ACT(small[:, 52:53], small[:, 51:52], AF.Exp, [sm], [sm], scale=-0.5)
                STT(og[:, :], po[:, :], small[:, 52:53], sr_h[:, blk, :], ALU.mult, ALU.mult, [pob, sm, srh_b], [og_b])
                PHASE[0] = "A2.upd"
                for i in range(nseg):
                    sidx = i if is_s else h
                    for c in range(2):
                        pu, pub = bank()
                        MM(pu[:, :], kd_tok[:, i, c * 128:(c + 1) * 128], v_h[:, blk, :], True, True, [kd_b, vh_b], [pub])
                        si = i if is_s else blk
                        STT(S32[:, sidx, c, :], S32[:, sidx, c, :], elv[:, c, si:si + 1], pu[:, :], ALU.mult, ALU.add,
                            [S_b, sm, pub], [S_b])
                        CP("act", Sbf[:, sidx, c, :], S32[:, sidx, c, :], [S_b], [Sbf_b])
                hoist_qk(h + 1, 1)
                PHASE[0] = "A2.tr2"
                pt2, pt2b = bank()
                pt2v = pt2[:, 0:256].bitcast(BF16).rearrange("p (i t) -> p i t", i=4)
                for i in range(4):
                    TR(pt2v[:, i, :], og[:, i * 128:(i + 1) * 128], ident_bf[:, :], [og_b, cb], [pt2b])
                for i in range(4):
                    fcg = 4 * h + i
                    ACT(o_glaT[:, fcg, cols], pt2v[:, i, :], AF.Copy, [pt2b, cb], [oglaT_b], scale=par16[:, fcg, 0:1])
            if is_s:
                for i in range(4):
                    DMA("sp", sgs_o[i, h].rearrange("(c p) v -> p c v", p=128), S32[:, i, :, :], S_b, [S_b], [])
        if last:
            for i in range(4):
                DMA("sp", sgp_o[seq, i].rearrange("(c p) v -> p c v", p=128), S32[:, i, :, :], S_b, [S_b], [])

        if DEBUG and is_s:
            DMA("sp", dbg2_o[:, 3072:8320], R1[:, 3072:8320], gla_b, [gla_b, vh_b, srh_b, attm_b, kd_b, og_b], [])
            DMA("sp", dbg3_o[:, :], R1[:, 0:1024].bitcast(F32), gla_b, [gla_b], [])
        if STOP == "A2":
            return
        PHASE[0] = "B"
        xlhs = lambda kk, s_: xT[:, kk, s_ * 128:(s_ + 1) * 128]
        for hp in range(4):
            (slot,), (sbuf_,) = wblock([(KC, 256, w_in_v[:, :, OQA + 256 * hp: OQA + 256 * (hp + 1)])])
            pt, pb = bank()
            ptv = pt[:, 0:2 * TT_].rearrange("p (c t) -> p c t", c=2)
            for j in range(2):
                for kc in range(KC):
                    MM(ptv[:, j, :], slot[:, kc, j * 128:(j + 1) * 128], xT[:, kc, 0:TT_],
                       kc == 0, kc == KC - 1, [sbuf_, xT_b], [pb])
            hh = hp * 2
            CP(cp_eng(), q_aT[:, hh:hh + 2, 0:TT_], ptv, [pb], [qaT_b])
        if STOP == "Bq":
            return
        for hp in range(4):
            (slot,), (sbuf_,) = wblock([(KC, 256, w_in_v[:, :, OKA + 256 * hp: OKA + 256 * (hp + 1)])])
            pt, pb = bank()
            ptv = pt[:, 0:2 * TT_].rearrange("p (c t) -> p c t", c=2)
            for j in range(2):
                for kc in range(KC):
                    MM(ptv[:, j, :], slot[:, kc, j * 128:(j + 1) * 128], xT[:, kc, 0:TT_],
                       kc == 0, kc == KC - 1, [sbuf_, xT_b], [pb])
            hh = hp * 2
            if is_s:
                CP(cp_eng(), kTn[:, hh:hh + 2, :], ptv, [pb], [newkv_b])
            else:
                CP(cp_eng(), Kring[:, slot3, hh:hh + 2, :], ptv, [pb], [K_b[slot3]])
            if kv_out:
                for s in range(NS):
                    pt, pb = bank()
                    for kc in range(KC):
                        MM(pt[:, 0:256], xT[:, kc, s * 128:(s + 1) * 128], slot[:, kc, :], kc == 0, kc == KC - 1,
                           [sbuf_, xT_b], [pb])
                    ki = kvst_rr[0] % 2
                    kvst_rr[0] += 1
                    CP(cp_eng(), kvst[:, ki, 0:256], pt[:, 0:256], [pb], [kvst_b[ki]])
                    if is_s:
                        dst = ks_o[:, hp * 256:(hp + 1) * 256]
                    else:
                        r0 = t0 - (plen - KEEP) + s * 128
                        dst = kp_o[seq, r0:r0 + 128, hp * 256:(hp + 1) * 256]
                    DMA("sp", dst, kvst[:, ki, 0:256], kvst_b[ki], [kvst_b[ki]], [])
        if STOP == "Bk":
            return
        for cbk in range(2):
            pbs_ = formA(w_in_v, OVA + 512 * cbk, KC, xlhs, [xT_b], NS)
            for s in range(NS):
                pt, pb = pbs_[s]
                ce = cp_eng()
                if is_s:
                    CP(ce, vNew[:, cbk * 512:(cbk + 1) * 512], pt[:, :], [pb], [newkv_b])
                else:
                    CP(ce, Vring[:, slot3, s, cbk * 512:(cbk + 1) * 512], pt[:, :], [pb], [V_b[slot3]])
                if kv_out:
                    ki = kvst_rr[0] % 2
                    kvst_rr[0] += 1
                    CP(ce, kvst[:, ki, :], pt[:, :], [pb], [kvst_b[ki]])
                    if is_s:
                        dst = vs_o[:, cbk * 512:(cbk + 1) * 512]
                    else:
                        r0 = t0 - (plen - KEEP) + s * 128
                        dst = vp_o[seq, r0:r0 + 128, cbk * 512:(cbk + 1) * 512]
                    DMA("sp", dst, kvst[:, ki, :], kvst_b[ki], [kvst_b[ki]], [])

        if STOP == "B":
            return
        PHASE[0] = "B2"
        if not is_s:
            items = [(pr, h) for pr in range(NS) for h in range(8)]

            def stage1(k, pr, h):
                gp = ti * 2 + pr
                blocks = [(b, gp - 4 + b) for b in range(5) if gp - 4 + b >= 0]
                pcols = slice(pr * 128, (pr + 1) * 128)
                low = [(b, g) for (b, g) in blocks if b < 4]
                if low:
                    bX, bXb = bank()
                bY, bYb = bank()
                if low:
                    b0 = low[0][0]
                    MM(bX[:, b0 * 128:512], ident_bf[:, :], biasPv[:, h, b0:4, :], True, False, [cb, bb], [bXb])
                    for (b, g) in low:
                        sl3 = (g // 2) % 3
                        MM(bX[:, b * 128:(b + 1) * 128], Kring[:, sl3, h, (g % 2) * 128:(g % 2 + 1) * 128],
                           q_aT[:, h, pcols], False, b == low[-1][0], [K_b[sl3], qaT_b], [bXb])
                MM(bY[:, 0:128], ident_bf[:, :], biasPv[:, h, 4, :], True, False, [cb, bb], [bYb])
                g = gp
                sl3 = (g // 2) % 3
                MM(bY[:, 0:128], Kring[:, sl3, h, (g % 2) * 128:(g % 2 + 1) * 128], q_aT[:, h, pcols], False, True,
                   [K_b[sl3], qaT_b], [bYb])
                pi = k % 2
                if low:
                    b0 = low[0][0]
                    ACT(pT[:, pi, b0:4, :], bX[:, b0 * 128:512].rearrange("p (b q) -> p b q", q=128), AF.Exp,
                        [bXb], [pT_b[pi]], scale=ATT_SCALE)
                ACT(pT[:, pi, 4, :], bY[:, 0:128], AF.Exp, [bYb], [pT_b[pi]], scale=ATT_SCALE)
                return (blocks, pcols, pi)

            def stage2(k, pr, h, st):
                blocks, pcols, pi = st
                bZ, bZb = bank()
                nb_ = len(blocks)
                for n, (b, g) in enumerate(blocks):
                    sl3 = (g // 2) % 3
                    MM(bZ[:, 0:128], Vring[:, sl3, g % 2, h * 128:(h + 1) * 128], pT[:, pi, b, :], n == 0, n == nb_ - 1,
                       [V_b[sl3], pT_b[pi]], [bZb])
                for n, (b, g) in enumerate(blocks):
                    MM(bZ[:, 128:256], ones_bf[:, :], pT[:, pi, b, :], n == 0, n == nb_ - 1, [cb, pT_b[pi]], [bZb])
                P.op("dve", lambda e: e.reciprocal(out=rc_t[:, pi, :], in_=bZ[:, 128:256]), [bZb], [rc_b[pi]])
                TT(o_attT[:, h, pcols], bZ[:, 0:128], rc_t[:, pi, :], ALU.mult, [bZb, rc_b[pi]], [oattT_b])

            prev = None
            for k, (pr, h) in enumerate(items):
                st = stage1(k, pr, h)
                if prev is not None:
                    stage2(*prev)
                prev = (k, pr, h, st)
            stage2(*prev)
        else:
            sb_b = buf("biasS")
            DMA("sp", btmp[:, 0:1024], biasSc_d[:, :], bt, [], [bt])
            TS(biasT[:, 0:1024], btmp[:, 0:1024], SQRT128, None, ALU.mult, None, [bt], [bb, bt])
            for h in range(8):
                DMA("sp", btmp[:, 0:128], biasSn_d[:, h * 128:(h + 1) * 128], bt, [], [bt])
                DMA("sp", btmp[:, 128:256], maskSn_d[:, :], bt, [], [bt])
                STT(biasT[:, 1024 + h * 128: 1024 + (h + 1) * 128], btmp[:, 0:128], SQRT128, btmp[:, 128:256],
                    ALU.mult, ALU.add, [bt], [bb, bt])
            kc_b = buf("KTc"); vc_b = buf("Vc"); kst_b = buf("Kst")
            for sq in range(4):
                DMA("pool", Kst[:, :, :], ck[sq].rearrange("(b p) f -> p b f", p=128), kst_b, [], [kst_b] + K_b + V_b)
                DMA("pool", Vc[:, :, :], cv[sq].rearrange("(b p) f -> p b f", p=128), vc_b, [], [vc_b] + K_b + V_b)
                for h in range(8):
                    pt, pb = bank()
                    ptv = pt[:, 0:256].bitcast(BF16).rearrange("p (b t) -> p b t", b=4)
                    for b in range(4):
                        TR(ptv[:, b, :], Kst[:, b, h * 128:(h + 1) * 128], ident_bf[:, :], [kst_b, cb], [pb])
                    CP(cp_eng(), KTc[:, h, :], pt[:, 0:256].bitcast(BF16), [pb], [kc_b])
                qc = slice(sq * SL, (sq + 1) * SL)
                for h in range(8):
                    bX, bXb = bank()
                    bY, bYb = bank()
                    MM(bX[:, 0:128], ident_bf[:, :], biasScv[:, h, :, :], True, False, [cb, bb], [bXb])
                    for b in range(4):
                        MM(bX[:, b * 32:(b + 1) * 32], KTc[:, h, b * 128:(b + 1) * 128], q_aT[:, h, qc], False, b == 3,
                           [kc_b, qaT_b], [bXb])
                    MM(bY[:, 0:32], ident_bf[:, :], biasSnv[:, h, qc], True, False, [cb, bb], [bYb])
                    MM(bY[:, 0:32], kTn[:, h, :], q_aT[:, h, qc], False, True, [newkv_b, qaT_b], [bYb])
                    pi = h % 2
                    pTs = pT[:, pi, 0, :].rearrange("p (b q) -> p b q", b=4)
                    ACT(pTs, bX[:, 0:128].rearrange("p (b q) -> p b q", b=4), AF.Exp, [bXb], [pT_b[pi]], scale=ATT_SCALE)
                    ACT(pT[:, pi, 1, 0:32], bY[:, 0:32], AF.Exp, [bYb], [pT_b[pi]], scale=ATT_SCALE)
                    bZ, bZb = bank()
                    for b in range(4):
                        MM(bZ[:, 0:32], Vc[:, b, h * 128:(h + 1) * 128], pTs[:, b, :], b == 0, False, [vc_b, pT_b[pi]], [bZb])
                    MM(bZ[:, 0:32], vNew[:, h * 128:(h + 1) * 128], pT[:, pi, 1, 0:32], False, True, [newkv_b, pT_b[pi]], [bZb])
                    for b in range(4):
                        MM(bZ[:, 128:160], ones_bf[:, :], pTs[:, b, :], b == 0, False, [cb, pT_b[pi]], [bZb])
                    MM(bZ[:, 128:160], ones_bf[:, :], pT[:, pi, 1, 0:32], False, True, [cb, pT_b[pi]], [bZb])
                    P.op("dve", lambda e, bZ=bZ, pi=pi: e.reciprocal(out=rc_t[:, pi, 0:32], in_=bZ[:, 128:160]),
                         [bZb], [rc_b[pi]])
                    TT(o_attT[:, h, qc], bZ[:, 0:32], rc_t[:, pi, 0:32], ALU.mult, [bZb, rc_b[pi]], [oattT_b])

        if DEBUG and is_s:
            DMA("sp", dbg_o[:, 0:16, :], o_glaT[:, :, 0:128], oglaT_b, [oglaT_b, oattT_b], [])
            DMA("sp", dbg_o[:, 16:24, :], o_attT[:, :, 0:128], oglaT_b, [oglaT_b, oattT_b], [])
        if STOP == "B2":
            return
        for s in range(NS):
            DMA("sp", h32[:, s, :], x_rows[s], hb[s], [], [hb[s]])

        PHASE[0] = "C"
        for j in range(8):
            wm, bm = wblock([(KC, 256, w_in_v[:, :, OMG + 256 * j: OMG + 256 * (j + 1)]),
                                (KC, 256, w_in_v[:, :, OMA + 256 * j: OMA + 256 * (j + 1)])])
            for w in range(2):
                pt, pb = bank()
                ptv = pt[:, 0:2 * TT_].rearrange("p (c t) -> p c t", c=2)
                for i in range(2):
                    for kc in range(KC):
                        MM(ptv[:, i, :], wm[w][:, kc, i * 128:(i + 1) * 128], xT[:, kc, 0:TT_],
                           kc == 0, kc == KC - 1, [bm[w], xT_b], [pb])
                for i in range(2):
                    ACT(sg_t[:, w, i, 0:TT_], ptv[:, i, :], AF.Sigmoid, [pb, cb], [sgt_b],
                        bias=par16[:, 2 * j + i, 1 + w:2 + w])
            wb, bwb = wblock([(KC, 256, w_brg_v[:, :, 256 * j: 256 * (j + 1)]),
                                (8, 256, w_bra_v[:, :, 256 * j: 256 * (j + 1)])])
            pg_, pgb_ = bank()
            pa_, pab_ = bank()
            pgv_ = pg_[:, 0:2 * TT_].rearrange("p (c t) -> p c t", c=2)
            pav_ = pa_[:, 0:2 * TT_].rearrange("p (c t) -> p c t", c=2)
            for i in range(2):
                for kc in range(KC):
                    MM(pgv_[:, i, :], wb[0][:, kc, i * 128:(i + 1) * 128], o_glaT[:, kc, 0:TT_], kc == 0, kc == KC - 1,
                       [bwb[0], oglaT_b], [pgb_])
            for i in range(2):
                for kc in range(8):
                    MM(pav_[:, i, :], wb[1][:, kc, i * 128:(i + 1) * 128], o_attT[:, kc, 0:TT_], kc == 0, kc == 7,
                       [bwb[1], oattT_b], [pab_])
            TT(sg_t[:, 0, :, 0:TT_], sg_t[:, 0, :, 0:TT_], pgv_, ALU.mult, [sgt_b, pgb_], [sgt_b])
            TT(sg_t[:, 1, :, 0:TT_], sg_t[:, 1, :, 0:TT_], pav_, ALU.mult, [sgt_b, pab_], [sgt_b])
            TT(mergedT[:, 2 * j:2 * j + 2, 0:TT_], sg_t[:, 0, :, 0:TT_], sg_t[:, 1, :, 0:TT_], ALU.add, [sgt_b], [mrg_b])

        if STOP == "C":
            return
        PHASE[0] = "D"
        for n in range(4):
            mlhs = lambda kk, s_: mergedT[:, kk, s_ * 128:(s_ + 1) * 128]
            pbs_ = formA(w_out_v, 512 * n, KC, mlhs, [mrg_b], NS)
            for s in range(NS):
                pt, pb = pbs_[s]
                hsl = h32[:, s, n * 512:(n + 1) * 512]
                STT(hsl, hsl, ALPHA, pt[:, :], ALU.mult, ALU.add, [hb[s], pb], [hb[s]])
        def ln1_transposes():
            for s in range(NS):
                for g4 in range(4):
                    pt, pb = bank()
                    ptv = pt[:, :].rearrange("p (i t) -> p i t", i=4)
                    for i in range(4):
                        fc = g4 * 4 + i
                        TR(ptv[:, i, :], h32[:, s, fc * 128:(fc + 1) * 128], ident_f[:, :], [hb[s], cb], [pb])
                    for i in range(4):
                        fc = g4 * 4 + i
                        ACT(hT[:, fc, s * 128:(s + 1) * 128], ptv[:, i, :], AF.Identity, [pb, cb], [xT_b],
                            bias=par16[:, fc, 4:5], scale=par16[:, fc, 3:4])

        layernorm_all(NS, "g1", "b1", 64, mid=ln1_transposes)
        if STOP == "D":
            return
        PHASE[0] = "E"
        L = SL if is_s else TT_
        for j in range(22):
            ncj = 2 if j < 21 else 1
            wu, bwu = wblock([(KC, ncj * 128, w_up_v[:, :, 256 * j: 256 * j + ncj * 128]),
                                (KC, ncj * 128, w_up_v[:, :, DFF + 256 * j: DFF + 256 * j + ncj * 128])])
            pu_, pub_ = bank()
            pg_, pgb_ = bank()
            puv = pu_[:, 0:2 * TT_].rearrange("p (c t) -> p c t", c=2)
            pgv_ = pg_[:, 0:2 * TT_].rearrange("p (c t) -> p c t", c=2)
            for i in range(ncj):
                for kc in range(KC):
                    MM(puv[:, i, :], wu[0][:, kc, i * 128:(i + 1) * 128], hT[:, kc, 0:TT_], kc == 0, kc == KC - 1,
                       [bwu[0], xT_b], [pub_])
            for i in range(ncj):
                for kc in range(KC):
                    MM(pgv_[:, i, :], wu[1][:, kc, i * 128:(i + 1) * 128], hT[:, kc, 0:TT_], kc == 0, kc == KC - 1,
                       [bwu[1], xT_b], [pgb_])
            for i in range(ncj):
                fc = 2 * j + i
                fi = fc % 3
                fb = ffn_b[fi]
                base = fi * 1024
                uext = ffn_tmp[:, base: base + nseg * (L + 2)].rearrange("p (s l) -> p s l", s=nseg)
                t1 = ffn_tmp[:, base + 264: base + 264 + TT_].rearrange("p (s l) -> p s l", s=nseg)
                gl = ffn_tmp[:, base + 520: base + 520 + TT_]
                al_w = [oglaT_b, oattT_b] if fc < 3 else []
                al_r = [oglaT_b, oattT_b] if fc >= NFC - 3 else []
                CP("act", uext[:, :, 2:L + 2], puv[:, i, :].rearrange("p (s l) -> p s l", s=nseg), [pub_], [fb] + al_w)
                CP("act", uext[:, :, 0:2], u_prev[:, fc, 0:nseg, :], [up_b], [fb])
                TS(t1, uext[:, :, 0:L], convp[:, fc, 0:1], convp[:, fc, 3:4], ALU.mult, ALU.add, [fb, cb], [fb])
                STT(t1, uext[:, :, 1:L + 1], convp[:, fc, 1:2], t1, ALU.mult, ALU.add, [fb, cb], [fb])
                STT(t1, uext[:, :, 2:L + 2], convp[:, fc, 2:3], t1, ALU.mult, ALU.add, [fb, cb], [fb])
                CP("act", u_prev[:, fc, 0:nseg, :], uext[:, :, L:L + 2], [fb], [up_b])
                ACT(gl, ffn_tmp[:, base + 264: base + 264 + TT_], AF.Gelu_apprx_tanh, [fb], [fb])
                al_a = [gla_b, vh_b, srh_b, attm_b, kd_b, og_b, qaT_b] if fc == 0 else []
                TT(actT[:, fc, 0:TT_], gl, pgv_[:, i, :], ALU.mult, [fb, pgb_] + al_r, [act_b] + al_a)
        if last:
            conv_out(cp_o[seq], 1)
        if is_s:
            conv_out(cs_o.rearrange("s t f -> (s t) f"), 4)

        if STOP == "E":
            return
        PHASE[0] = "F"
        alhs = lambda kk, s_: actT[:, kk, s_ * 128:(s_ + 1) * 128]
        for n in range(4):
            pbs = formA(w_dn_v, 512 * n, NFC, alhs, [act_b], NS)
            for s in range(NS):
                pt, pb = pbs[s]
                hsl = h32[:, s, n * 512:(n + 1) * 512]
                STT(hsl, hsl, ALPHA, pt[:, :], ALU.mult, ALU.add, [hb[s], pb], [hb[s]])
        layernorm_all(NS, "g2", "b2", 80,
                      after=lambda s: DMA("sp", y_rows[s], h32[:, s, :], hb[s], [hb[s]], []))

    if STOP != "setup":
        for seq in range(n_pseq):
            for ti in range(NT):
                tile("p", seq, ti)
        if has_sample:
            tile("s", 0, 0)

    for e in ("pe", "act", "dve"):
        c = 0
        for ins in P.lists[e]:
            if ins.need_inc:
                c += 1
                ins.cnt = c
    sems = {e: es.enter_context(nc.semaphore(f"sem_{e}")) for e in ("pe", "act", "dve")}
    for b_ in P.dma_bufs:
        b_.sem = es.enter_context(nc.semaphore(f"d_{b_.name}"))

    engmap = {"pe": "tensor", "act": "scalar", "dve": "vector", "pool": "gpsimd", "sp": "sync"}

    def emit(ename):
        def body(eng):
            known = {}
            for ins in P.lists[ename]:
                need = {}
                for d in ins.deps:
                    if d.is_dma:
                        key, val, sem = id(d.sembuf), d.cum, d.sembuf.sem
                    else:
                        key, val, sem = d.eng, d.cnt, sems[d.eng]
                    if key not in need or need[key][0] < val:
                        need[key] = (val, sem)
                for key, (val, sem) in need.items():
                    if known.get(key, 0) >= val:
                        continue
                    known[key] = val
                    eng.wait_ge(sem, val)
                r = ins.fn(eng)
                if ename == "pe":
                    NAME2TAG[r.ins.name] = ins.tag
                if ins.is_dma:
                    r.then_inc(ins.sembuf.sem, 16)
                elif ins.need_inc:
                    r.then_inc(sems[ename], 1)
            if ename == "sp":
                for b_ in P.dma_bufs:
                    eng.wait_ge(b_.sem, b_.dcount)
        return body

    with nc.Block() as block:
        block.tensor(emit("pe"))
        block.scalar(emit("act"))
        block.vector(emit("dve"))
        block.gpsimd(emit("pool"))
        block.sync(emit("sp"))
    es.close()
    return nc


def _const_tables():
    ident = np.eye(128, dtype=np.float32)
    s = np.arange(128)
    cmaskP = (s[:, None] <= s[None, :]).astype(np.float32)
    same = (s[:, None] // 32) == (s[None, :] // 32)
    cmaskS = (cmaskP * same).astype(np.float32)
    rowmask = np.zeros((128, 4), np.float32)
    for i in range(4):
        rowmask[i * 32:(i + 1) * 32, i] = 1.0
    j = np.arange(128)[:, None, None]
    b = np.arange(5)[None, :, None]
    q = np.arange(128)[None, None, :]
    kch = 2 * b + j // 64 - 8
    qch = q // 64
    valid = (kch >= qch - 8) & (kch <= qch)
    maskP = np.where(valid, 0.0, -1e30).astype(np.float32).reshape(128, 5 * 128)
    maskSn = np.where(same, 0.0, -1e30).astype(np.float32)
    return ident, cmaskP, cmaskS, rowmask, maskP, maskSn


def _bias_indices():
    j = np.arange(128)[:, None, None]
    b = np.arange(5)[None, :, None]
    q = np.arange(128)[None, None, :]
    relP = q + 512 - 128 * b - j
    idxP = np.clip(relP, -128, 128) + 128
    b4 = np.arange(4)[None, :, None]
    i = np.arange(32)[None, None, :]
    relC = 512 + i - 128 * b4 - j
    idxC = np.clip(relC, -128, 128) + 128
    jj = np.arange(128)[:, None]
    qq = np.arange(128)[None, :]
    relN = (qq % 32) - (jj % 32)
    idxN = np.clip(relN, -128, 128) + 128
    return idxP, idxC, idxN


_NC_CACHE = {}


def _get_nc(key):
    if key not in _NC_CACHE:
        _NC_CACHE[key] = build_program(*key)
    return _NC_CACHE[key]


def run_cores(inputs, n_cores, n_pseq, plen, n_sseq):
    f = lambda a: np.ascontiguousarray(np.asarray(a, dtype=np.float32))
    x_prompt = f(inputs["x_prompt"]); x_sample = f(inputs["x_sample"])
    ident, cmaskP, cmaskS, rowmask, maskP, maskSn = _const_tables()
    idxP, idxC, idxN = _bias_indices()
    tab = f(inputs["att_rel_bias"])[0]
    biasP = np.ascontiguousarray(tab[:, idxP].transpose(1, 0, 2, 3).reshape(128, 8 * 5 * 128))
    biasSc = np.ascontiguousarray(tab[:, idxC].transpose(1, 0, 2, 3).reshape(128, 8 * 4 * 32))
    biasSn = np.ascontiguousarray(tab[:, idxN].transpose(1, 0, 2).reshape(128, 8 * 128))
    shared = {
        "w_in": f(inputs["w_in"])[0], "gate_up": f(inputs["gla_gate_up"])[0], "gate_b": f(inputs["gla_gate_b"])[0],
        "gnorm": f(inputs["gla_norm_g"])[0], "merge_b": f(inputs["merge_b"])[0],
        "w_brg": f(inputs["w_br_gla"])[0], "w_bra": f(inputs["w_br_att"])[0], "w_out": f(inputs["w_out"])[0],
        "ln1g": f(inputs["ln1_g"])[0], "ln1b": f(inputs["ln1_b"])[0], "w_up": f(inputs["w_ffn_up"])[0],
        "convw": f(inputs["ffn_conv_w"])[0], "convb": f(inputs["ffn_conv_b"])[0], "w_dn": f(inputs["w_ffn_down"])[0],
        "ln2g": f(inputs["ln2_g"])[0], "ln2b": f(inputs["ln2_b"])[0],
        "biasP": biasP, "maskP": maskP, "biasSc": biasSc, "biasSn": biasSn, "maskSn": maskSn,
        "ident": ident, "cmaskP": cmaskP, "cmaskS": cmaskS, "rowmask": rowmask,
    }
    ck = f(inputs["cache_att_k"])[0]; cv = f(inputs["cache_att_v"])[0]
    sg = f(inputs["state_gla"])[0]; sc = f(inputs["state_ffn_conv"])[0]
    in_maps = []
    for c in range(n_cores):
        m = dict(shared)
        m["xp"] = np.ascontiguousarray(x_prompt[c * n_pseq:(c + 1) * n_pseq])
        if n_sseq:
            sl = slice(c * 4, (c + 1) * 4)
            m["xs"] = np.ascontiguousarray(x_sample[sl].reshape(128, D))
            m["ck"] = np.ascontiguousarray(ck[sl].reshape(4, 512, 1024))
            m["cv"] = np.ascontiguousarray(cv[sl].reshape(4, 512, 1024))
            m["sg"] = np.ascontiguousarray(sg[sl]); m["sc"] = np.ascontiguousarray(sc[sl])
        else:
            m["xs"] = np.zeros((128, D), np.float32)
            m["ck"] = np.zeros((4, 512, 1024), np.float32); m["cv"] = np.zeros((4, 512, 1024), np.float32)
            m["sg"] = np.zeros((4, 4, 256, 512), np.float32); m["sc"] = np.zeros((4, 2, DFF), np.float32)
        in_maps.append(m)
    nc = _get_nc((n_pseq, plen, n_sseq))
    res = run_bass_kernel_spmd(nc, in_maps, core_ids=list(range(n_cores)))
    R = res.results
    keep = min(512, plen)
    cat = lambda k: np.concatenate([r[k] for r in R], axis=0)
    y_p = cat("y_p")
    y_s = cat("y_s").reshape(n_cores * 4, 32, D)
    kp = cat("kp").reshape(1, n_cores * n_pseq, keep, 8, 128)
    vp = cat("vp").reshape(1, n_cores * n_pseq, keep, 8, 128)
    sgp = cat("sgp")[None]
    cp = cat("cp")[None]
    ks = cat("ks").reshape(1, n_cores * 4, 32, 8, 128)
    vs = cat("vs").reshape(1, n_cores * 4, 32, 8, 128)
    sgs = cat("sgs")[None]
    cs = cat("cs")[None]
    return (y_p, y_s, kp, vp, sgp, cp, ks, vs, sgs, cs)


def kernel(**inputs):
    return run_cores(inputs, NCORES, 2, 2048, 4)
```
